# Optimizing a Trainium2 kernel written in Bass

```python
import jax, jax.numpy as jnp
from jax import lax
import numpy as np

D_MODEL = 2048
BATCH = 8
SEQ = 4096
DEPTH = 1
DEC_BATCH = 4
DEC_SEQ = 4096
PAST_LEN = 128

GRID_W = 64
HEAD_DIM = 128
NA_HEADS = 8
GQA_HEADS = 8
GQA_KV_HEADS = 2
NA_WIDTH = NA_HEADS * HEAD_DIM
GQA_WIDTH = GQA_HEADS * HEAD_DIM
KV_WIDTH = GQA_KV_HEADS * HEAD_DIM
NA_ROWS_MAX = 8
NA_COLS = 16
D_FF = 4 * D_MODEL
PLE_DIM = 256
ROPE_THETA = 10000.0
Q_BLOCK = 128
EPS = 1e-6
NEG_INF = -1e30
IN_COLS = 3 * NA_WIDTH + GQA_WIDTH + 2 * KV_WIDTH + 2 * D_MODEL

kernel_name = "hybrid_natten_gqa_encoder"


def rms_norm(x, g):
    xf = x.astype(jnp.float32)
    y = xf * lax.rsqrt(jnp.mean(xf * xf, axis=-1, keepdims=True) + EPS)
    return (y * g.astype(jnp.float32)).astype(x.dtype)


def neighbourhood_attention(q, k, v, rpb):
    B, S, H, Dh = q.shape
    rows = S // GRID_W
    kr = min(NA_ROWS_MAX, rows)
    r = jnp.arange(rows)
    c = jnp.arange(GRID_W)
    row_start = jnp.clip(r - kr // 2, 0, rows - kr)
    key_rows = row_start[:, None] + jnp.arange(kr)[None, :]
    col_start = jnp.clip(c - NA_COLS // 2, 0, GRID_W - NA_COLS)
    qg = q.reshape(B, rows, GRID_W, H, Dh)
    kg = k.reshape(B, rows, GRID_W, H, Dh)[:, key_rows]
    vg = v.reshape(B, rows, GRID_W, H, Dh)[:, key_rows]
    scale = Dh ** -0.5
    s = jnp.einsum('brqhd,brjkhd->brhqjk', qg.astype(jnp.float32) * scale,
                   kg.astype(jnp.float32))
    dr = key_rows - r[:, None]
    dc = jnp.clip(c[None, :] - c[:, None], -(NA_COLS - 1), NA_COLS - 1)
    bias = rpb[:, (dr + NA_ROWS_MAX - 1)[:, :, None, None],
               (dc + NA_COLS - 1)[None, None, :, :]]
    bias = bias.transpose(1, 0, 3, 2, 4).astype(jnp.float32)
    col_ok = (c[None, :] >= col_start[:, None]) & (c[None, :] < col_start[:, None] + NA_COLS)
    s = jnp.where(col_ok[:, None, :], s + bias[None], NEG_INF)
    p = jax.nn.softmax(s, axis=(-2, -1))
    out = jnp.einsum('brhqjk,brjkhd->brqhd', p.astype(v.dtype), vg)
    return out.reshape(B, S, H * Dh)


def rope_1d(x, pos):
    d = x.shape[-1]
    freqs = ROPE_THETA ** (-jnp.arange(0, d, 2, dtype=jnp.float32) / d)
    ang = pos.astype(jnp.float32)[:, None] * freqs[None, :]
    cos = jnp.cos(ang)[:, None, :]
    sin = jnp.sin(ang)[:, None, :]
    xf = x.astype(jnp.float32)
    x1, x2 = xf[..., : d // 2], xf[..., d // 2:]
    return jnp.concatenate([x1 * cos - x2 * sin, x2 * cos + x1 * sin], axis=-1).astype(x.dtype)


def axial_rope(x):
    S = x.shape[1]
    t = jnp.arange(S)
    half = x.shape[-1] // 2
    return jnp.concatenate([rope_1d(x[..., :half], t // GRID_W),
                            rope_1d(x[..., half:], t % GRID_W)], axis=-1)


def gqa_attention(q, k, v):
    B, S, Hq, Dh = q.shape
    Hkv = k.shape[2]
    G = Hq // Hkv
    nblk = S // Q_BLOCK
    scale = Dh ** -0.5
    qb = q.reshape(B, nblk, Q_BLOCK, Hkv, G, Dh).transpose(1, 0, 2, 3, 4, 5)
    kf = k.astype(jnp.float32)

    def attend(qblk):
        s = jnp.einsum('bqkgd,bskd->bkgqs', qblk.astype(jnp.float32) * scale, kf)
        p = jax.nn.softmax(s, axis=-1)
        return jnp.einsum('bkgqs,bskd->bqkgd', p.astype(v.dtype), v)

    out = lax.map(attend, qb)
    return out.transpose(1, 0, 2, 3, 4, 5).reshape(B, S, Hq * Dh)


def trunk(x, p, ln_mix_pre, w_in, q_norm, k_norm, na_rpb, w_na_branch, w_gqa_branch,
          w_mix_out, ln_mix_post, ln_mlp_pre, w_ff1, w_ff2, ln_mlp_post,
          ln_ple_pre, w_ple_gate, w_ple_proj, ln_ple_post):
    B, S, _ = x.shape
    splits = np.cumsum([NA_WIDTH, NA_WIDTH, NA_WIDTH, GQA_WIDTH, KV_WIDTH, KV_WIDTH, D_MODEL])
    for i in range(DEPTH):
        h = rms_norm(x, ln_mix_pre[i])
        proj = h @ w_in[i]
        na_q, na_k, na_v, g_q, g_k, g_v, gate_a, gate_b = jnp.split(proj, splits, axis=-1)
        a = neighbourhood_attention(na_q.reshape(B, S, NA_HEADS, HEAD_DIM),
                                    na_k.reshape(B, S, NA_HEADS, HEAD_DIM),
                                    na_v.reshape(B, S, NA_HEADS, HEAD_DIM), na_rpb[i])
        gq = axial_rope(rms_norm(g_q.reshape(B, S, GQA_HEADS, HEAD_DIM), q_norm[i]))
        gk = axial_rope(rms_norm(g_k.reshape(B, S, GQA_KV_HEADS, HEAD_DIM), k_norm[i]))
        b = gqa_attention(gq, gk, g_v.reshape(B, S, GQA_KV_HEADS, HEAD_DIM))
        merged = (jax.nn.sigmoid(gate_a) * (a @ w_na_branch[i])
                  + jax.nn.sigmoid(gate_b) * (b @ w_gqa_branch[i]))
        x = x + rms_norm(merged @ w_mix_out[i], ln_mix_post[i])
        h = rms_norm(x, ln_mlp_pre[i])
        f = jnp.square(jax.nn.relu(h @ w_ff1[i])) @ w_ff2[i]
        x = x + rms_norm(f, ln_mlp_post[i])
        gate = jax.nn.sigmoid(rms_norm(x, ln_ple_pre[i]) @ w_ple_gate[i])
        e = (p[i] @ w_ple_proj[i]) * gate
        x = x + rms_norm(e, ln_ple_post[i])
    return x


def setup_inputs(seed: int = 0) -> dict:
    key = jax.random.key(seed)
    ks = jax.random.split(key, 24)
    f32 = jnp.float32

    def dense(k, fan_in, fan_out):
        return jax.random.normal(k, (DEPTH, fan_in, fan_out), f32) * fan_in ** -0.5

    def gain(k, n):
        return 1.0 + 0.05 * jax.random.normal(k, (DEPTH, n), f32)

    return {
        "x_prompt": jax.random.normal(ks[0], (BATCH, SEQ, D_MODEL), f32),
        "x_sample": jax.random.normal(ks[1], (DEC_BATCH, DEC_SEQ, D_MODEL), f32),
        "p_prompt": jax.random.normal(ks[2], (DEPTH, BATCH, SEQ, PLE_DIM), f32),
        "p_sample": jax.random.normal(ks[3], (DEPTH, DEC_BATCH, DEC_SEQ, PLE_DIM), f32),
        "ln_mix_pre": gain(ks[4], D_MODEL),
        "w_in": dense(ks[5], D_MODEL, IN_COLS),
        "q_norm": gain(ks[6], HEAD_DIM),
        "k_norm": gain(ks[7], HEAD_DIM),
        "na_rpb": 0.1 * jax.random.normal(ks[8], (DEPTH, NA_HEADS, 2 * NA_ROWS_MAX - 1, 2 * NA_COLS - 1), f32),
        "w_na_branch": dense(ks[9], NA_WIDTH, D_MODEL),
        "w_gqa_branch": dense(ks[10], GQA_WIDTH, D_MODEL),
        "w_mix_out": dense(ks[11], D_MODEL, D_MODEL),
        "ln_mix_post": gain(ks[12], D_MODEL),
        "ln_mlp_pre": gain(ks[13], D_MODEL),
        "w_ff1": dense(ks[14], D_MODEL, D_FF),
        "w_ff2": dense(ks[15], D_FF, D_MODEL),
        "ln_mlp_post": gain(ks[16], D_MODEL),
        "ln_ple_pre": gain(ks[17], D_MODEL),
        "w_ple_gate": dense(ks[18], D_MODEL, D_MODEL),
        "w_ple_proj": dense(ks[19], PLE_DIM, D_MODEL),
        "ln_ple_post": gain(ks[20], D_MODEL),
    }


def reference(x_prompt, x_sample, p_prompt, p_sample, ln_mix_pre, w_in, q_norm, k_norm, na_rpb,
              w_na_branch, w_gqa_branch, w_mix_out, ln_mix_post, ln_mlp_pre, w_ff1, w_ff2,
              ln_mlp_post, ln_ple_pre, w_ple_gate, w_ple_proj, ln_ple_post):
    y_prompt = trunk(x_prompt, p_prompt, ln_mix_pre, w_in, q_norm, k_norm, na_rpb, w_na_branch,
                     w_gqa_branch, w_mix_out, ln_mix_post, ln_mlp_pre, w_ff1, w_ff2, ln_mlp_post,
                     ln_ple_pre, w_ple_gate, w_ple_proj, ln_ple_post)
    y_sample = trunk(x_sample, p_sample, ln_mix_pre, w_in, q_norm, k_norm, na_rpb, w_na_branch,
                     w_gqa_branch, w_mix_out, ln_mix_post, ln_mlp_pre, w_ff1, w_ff2, ln_mlp_post,
                     ln_ple_pre, w_ple_gate, w_ple_proj, ln_ple_post)
    return (y_prompt, y_sample)
```

```python
import contextlib
import numpy as np
import ml_dtypes
import concourse.bass as bass
import concourse.mybir as mybir
from concourse.bass_utils import run_bass_kernel_spmd

F32 = mybir.dt.float32
BF16 = mybir.dt.bfloat16
AF = mybir.ActivationFunctionType
ALU = mybir.AluOpType

GRID = 64
SEQ = 4096
HALF = 2048
HD = 128
EPS = 1e-6
ROPE_THETA = 10000.0
T = 512
NSLOT = 22
MASKV = -30000.0


class Cfg:
    def __init__(self, D=2048, NAH=8, GQH=8, KVH=2, DFF=8192, PLE=256):
        self.D, self.NAH, self.GQH, self.KVH, self.DFF, self.PLE = D, NAH, GQH, KVH, DFF, PLE
        self.G = GQH // KVH
        self.KC = D // 128
        self.NAW = NAH * HD
        self.GQW = GQH * HD
        self.KVW = KVH * HD
        self.IN_COLS = 3 * self.NAW + self.GQW + 2 * self.KVW + 2 * D
        assert self.IN_COLS % 512 == 0 and D % 512 == 0 and DFF % 1024 == 0
        assert self.G * 128 == 512
        kinds = []
        kinds += [("naq", h) for h in range(NAH)]
        kinds += [("nak", h) for h in range(NAH)]
        kinds += [("nav", h) for h in range(NAH)]
        kinds += [("gq", h) for h in range(GQH)]
        kinds += [("gk", h) for h in range(KVH)]
        kinds += [("gv", h) for h in range(KVH)]
        kinds += [("ga", m) for m in range(self.KC)]
        kinds += [("gb", m) for m in range(self.KC)]
        self.kinds = kinds
        self.TOK_OWN = SEQ + HALF
        self.NA_LOC = SEQ + HALF + 512
        self.GK_LOC = SEQ + SEQ
        self.N_XALL = SEQ + HALF + HALF + 512


class Prod:
    __slots__ = ("sem", "step", "count", "name")

    def __init__(self, sem, step, name=""):
        self.sem, self.step, self.count, self.name = sem, step, 0, name


class Res:
    __slots__ = ("name", "w", "r")

    def __init__(self, name=""):
        self.name, self.w, self.r = name, None, {}


class Eng:
    def __init__(self, name, sem, is_pe=False):
        self.name = name
        self.prod = Prod(sem, 1, name)
        self.seen = {}
        self.is_pe = is_pe
        self.ops = []
        self.dma_prods = []
        self.dma_rr = 0

    def _need(self, p, c):
        if p is self.prod and self.is_pe:
            return
        if self.seen.get(p, 0) >= c:
            return
        assert p.count >= c, f"forward wait {self.name} on {p.name}: {p.count} < {c}"
        self.ops.append(("w", p.sem, c * p.step))
        self.seen[p] = c

    def _sync(self, reads, writes):
        need = {}
        for r in reads:
            if r.w is not None:
                p, c = r.w
                if c > need.get(p, 0):
                    need[p] = c
        for w in writes:
            if w.w is not None:
                p, c = w.w
                if c > need.get(p, 0):
                    need[p] = c
            for p, c in w.r.items():
                if c > need.get(p, 0):
                    need[p] = c
        for p, c in need.items():
            self._need(p, c)

    def op(self, fn, reads=(), writes=(), inc=True):
        self._sync(reads, writes)
        if inc:
            self.prod.count += 1
            c = self.prod.count
            self.ops.append(("i", fn, self.prod.sem, 1))
        else:
            c = self.prod.count + 1
            self.ops.append(("i", fn, None, 0))
        for r in reads:
            r.r[self.prod] = c
        for w in writes:
            w.w = (self.prod, c)
            w.r = {}

    def dma(self, out_ap, in_ap, reads=(), writes=(), **kw):
        P = self.dma_prods[self.dma_rr % len(self.dma_prods)]
        self.dma_rr += 1
        if P.count > 0:
            self._need(P, P.count)
        self._sync(reads, writes)
        P.count += 1
        self.ops.append(("i", lambda e, o=out_ap, i=in_ap, kw=kw: e.dma_start(out=o, in_=i, **kw), P.sem, 16))
        for r in reads:
            r.r[P] = P.count
        for w in writes:
            w.w = (P, P.count)
            w.r = {}

    def replay(self, h):
        for o in self.ops:
            if o[0] == "w":
                h.wait_ge(o[1], o[2])
            else:
                inst = o[1](h)
                if o[2] is not None:
                    inst.then_inc(o[2], o[3])


def _rope_tables(pos):
    pos = np.asarray(pos)
    p = np.arange(128)
    w = p % 64
    f = w % 32
    freqs = (ROPE_THETA ** (-(2.0 * f.astype(np.float32)) / np.float32(64))).astype(np.float32)
    pr = (pos // GRID).astype(np.float32)
    pc = (pos % GRID).astype(np.float32)
    pp = np.where((p < 64)[:, None], pr[None, :], pc[None, :]).astype(np.float32)
    ang = (pp * freqs[:, None]).astype(np.float32)
    cos = np.cos(ang).astype(np.float32)
    sin = np.sin(ang).astype(np.float32)
    sgn = np.where(w < 32, -1.0, 1.0).astype(np.float32)[:, None]
    return cos, (sin * sgn).astype(np.float32)


def _perm_T():
    m = np.arange(128)
    src = np.where((m % 64) < 32, m + 32, m - 32)
    P = np.zeros((128, 128), np.float32)
    P[src, m] = 1.0
    return P


def _na_tables():
    a = np.arange(128) // 64
    kc = np.arange(128) % 64
    s = np.arange(NSLOT * 64) // 64
    qc = np.arange(NSLOT * 64) % 64
    dr = (10 - s)[None, :] + a[:, None]
    dc = kc[:, None] - qc[None, :]
    cs = np.clip(qc - 8, 0, GRID - 16)
    colv = (kc[:, None] >= cs[None, :]) & (kc[:, None] < cs[None, :] + 16)
    valid = (np.abs(dr) <= 7) & colv
    return np.clip(dr + 7, 0, 14), np.clip(dc + 15, 0, 30), valid


def _row_valid(bt):
    v = np.zeros((16, 8), bool)
    for i in range(8):
        st = {"int": i, "first": max(i, 4), "last": min(i, 4)}[bt]
        v[st:st + 8, i] = True
    return v


def _rowB(types):
    out = np.zeros((128, len(types), 512), np.float32)
    for b, bt in enumerate(types):
        v = _row_valid(bt)
        out[:16, b, :] = np.repeat(np.where(v, 0.0, MASKV), 64, axis=1)
    return out.astype(ml_dtypes.bfloat16)


def _A16():
    A = np.zeros((128, 8, 128), np.float32)
    for j in range(8):
        for key in range(128):
            A[2 * j + key // 64, j, key] = 1.0
    return A.astype(ml_dtypes.bfloat16)


def build_program(cfg):
    D, KC, NAH, GQH, KVH, G, DFF, PLE = cfg.D, cfg.KC, cfg.NAH, cfg.GQH, cfg.KVH, cfg.G, cfg.DFF, cfg.PLE
    NT_IN = cfg.IN_COLS // 512
    NT_D = D // 512
    PKC = PLE // 128
    NAKC = cfg.NAW // 128
    GQKC = cfg.GQW // 128
    FF_HALF = DFF // 2
    UCH = FF_HALF // 128
    KGS = min(16, UCH)
    KG = UCH // KGS
    assert UCH % KGS == 0
    HG = min(4, NAH)
    NHG = NAH // HG
    SCALE = float(HD) ** -0.5

    nc = bass.Bass("TRN2", target_bir_lowering=False)

    def din(name, shape, dt=F32):
        return nc.dram_tensor(name, list(shape), dt, kind="ExternalInput").ap()

    def dscr(name, shape, dt=BF16):
        return nc.dram_tensor(name, list(shape), dt, kind="Internal").ap()

    xall = din("xall", [cfg.N_XALL, D])
    pown = din("pown", [cfg.TOK_OWN, PLE])
    w_in = din("w_in", [D, cfg.IN_COLS])
    w_na = din("w_na", [cfg.NAW, D])
    w_gqa = din("w_gqa", [cfg.GQW, D])
    w_mix = din("w_mix", [D, D])
    w_ff1 = din("w_ff1", [D, DFF])
    w_ff2 = din("w_ff2", [DFF, D])
    w_gate = din("w_gate", [D, D])
    w_proj = din("w_proj", [PLE, D])
    gains = {n: din(n, [D]) for n in ["ln_mix_pre", "ln_mix_post", "ln_mlp_pre", "ln_mlp_post", "ln_ple_pre",
                                      "ln_ple_post"]}
    q_norm = din("q_norm", [128])
    k_norm = din("k_norm", [128])
    rpbG = din("rpbG", [NAH, 128, NSLOT * 64])
    maskK = din("maskK", [128, NSLOT * 64], BF16)
    ropecs = din("ropecs", [2, 128, cfg.GK_LOC])
    rowB_d = din("rowB", [128, 12, 512], BF16)
    A16_d = din("A16", [128, 8, 128], BF16)
    ident_d = din("ident", [128, 128], BF16)
    permT_d = din("permT", [128, 128])
    out = nc.dram_tensor("out", [cfg.TOK_OWN, D], F32, kind="ExternalOutput").ap()

    wb = {
        "in": dscr("wb_in", [NT_IN, 128, KC, 512]),
        "na": dscr("wb_na", [NT_D, 128, NAKC, 512]),
        "gqa": dscr("wb_gqa", [NT_D, 128, GQKC, 512]),
        "mix": dscr("wb_mix", [NT_D, 128, KC, 512]),
        "ff1": dscr("wb_ff1", [DFF // 512, 128, KC, 512]),
        "ff2": dscr("wb_ff2", [NT_D, 128, DFF // 128, 512]),
        "gate": dscr("wb_gate", [NT_D, 128, KC, 512]),
        "proj": dscr("wb_proj", [NT_D, 128, PKC, 512]),
    }
    wsrc = {"in": w_in, "na": w_na, "gqa": w_gqa, "mix": w_mix, "ff1": w_ff1, "ff2": w_ff2, "gate": w_gate,
            "proj": w_proj}
    naqT = dscr("naqT", [NAH, 128, cfg.TOK_OWN])
    nakT = dscr("nakT", [NAH, 128, cfg.NA_LOC])
    nav = dscr("nav", [cfg.NA_LOC, cfg.NAW])
    gqT = dscr("gqT", [GQH, 128, cfg.TOK_OWN])
    gkT = dscr("gkT", [KVH, 128, cfg.GK_LOC])
    gv = dscr("gv", [cfg.GK_LOC, cfg.KVW])
    sgaT = dscr("sgaT", [KC, 128, cfg.TOK_OWN])
    sgbT = dscr("sgbT", [KC, 128, cfg.TOK_OWN])
    aT_d = naqT
    bT_d = gqT

    es = contextlib.ExitStack()
    with es:
        def sb(name, shape, dt):
            return es.enter_context(nc.sbuf_tensor("sb_" + name, list(shape), dt))

        def sem(name):
            return es.enter_context(nc.semaphore(name))

        PE = Eng("pe", sem("s_pe"), is_pe=True)
        ACT = Eng("act", sem("s_act"))
        DVE = Eng("dve", sem("s_dve"))
        POOL = Eng("pool", sem("s_pool"))
        SP = Eng("sp", sem("s_sp"))
        SP.dma_prods = [Prod(sem(f"s_spd{i}"), 16, f"spd{i}") for i in range(40)]
        POOL.dma_prods = [Prod(sem(f"s_pld{i}"), 16, f"pld{i}") for i in range(40)]
        engines = [PE, ACT, DVE, POOL, SP]

        def barrier():
            prods = [e.prod for e in engines] + SP.dma_prods + POOL.dma_prods
            for e in engines:
                for p in prods:
                    if p.count > 0 and not (p is e.prod) and e.seen.get(p, 0) < p.count:
                        e.ops.append(("w", p.sem, p.count * p.step))
                        e.seen[p] = p.count

        ident = sb("ident", [128, 128], BF16)
        permT = sb("permT", [128, 128], F32)
        onesf = sb("onesf", [128, 128], F32)
        onesb = sb("onesb", [128, 128], BF16)
        ones1 = sb("ones1", [128, 128], F32)
        gsb = {n: sb("g_" + n, [128, KC], F32) for n in ["ln_mix_pre", "ln_mlp_pre", "ln_ple_pre"]}
        qn = sb("qn", [128, 1], F32)
        kn = sb("kn", [128, 1], F32)
        NWS = 3
        wring = [sb(f"wring{i}", [128, 16, 512], BF16) for i in range(NWS)]
        r_wring = [Res(f"wring{i}") for i in range(NWS)]
        r_wring2 = [Res(f"wring2_{i}") for i in range(NWS)]
        stat = sb("stat", [128, 16], F32)
        r_stat = Res("stat")
        r_stat_a = [Res(f"stat_a{i}") for i in range(4)]
        r_stat_b = [Res(f"stat_b{i}") for i in range(4)]
        r_const = Res("const")
        ps = [es.enter_context(nc.psum_tensor(f"ps{i}", [128, 512], F32)) for i in range(8)]
        r_ps = [Res(f"ps{i}") for i in range(8)]
        bank_rr = [0]
        bank_mod = [8, 0]

        def bank():
            i = bank_mod[1] + bank_rr[0] % bank_mod[0]
            bank_rr[0] += 1
            return ps[i], r_ps[i]

        ARENA_BYTES = 154 * 1024
        arena = sb("arena", [128, ARENA_BYTES // 2], BF16)

        class Carver:
            def __init__(self):
                self.off = 0

            def take(self, shape, dt):
                n = int(np.prod(shape))
                esz = 2 if dt == BF16 else 4
                self.off = (self.off + 63) // 64 * 64
                o = self.off
                self.off += n * esz
                assert self.off <= ARENA_BYTES, f"arena overflow {self.off}"
                a = arena[:, o // 2:(o + n * esz) // 2]
                if dt == F32:
                    a = a.bitcast(F32)
                if len(shape) == 2:
                    a = a.rearrange("p (a b) -> p a b", b=shape[1])
                elif len(shape) == 3:
                    a = a.rearrange("p (a b c) -> p a b c", b=shape[1], c=shape[2])
                return a

        SP.dma(ident[:], ident_d, writes=[r_const])
        SP.dma(permT[:], permT_d, writes=[r_const])
        SP.dma(qn[:], q_norm.rearrange("(p o) -> p o", o=1), writes=[r_const])
        SP.dma(kn[:], k_norm.rearrange("(p o) -> p o", o=1), writes=[r_const])
        for n in gsb:
            SP.dma(gsb[n][:], gains[n].rearrange("(kc p) -> p kc", p=128), writes=[r_const],
                   allow_slow_non_contiguous=True)
        DVE.op(lambda e: e.memset(onesf[:], 1.0 / 128), writes=[r_const])
        DVE.op(lambda e: e.memset(onesb[:], 1.0), writes=[r_const])
        DVE.op(lambda e: e.memset(ones1[:], 1.0), writes=[r_const])

        r_wb = {}
        for name in wb:
            nt_n, _, kct, _ = wb[name].shape
            r_wb[name] = [[Res(f"wb_{name}{i}_{g}") for g in range((kct + 15) // 16)] for i in range(nt_n)]

        def cast_list(name):
            w, dst = wsrc[name], wb[name]
            nt_n, _, kct, _ = dst.shape
            src = w.rearrange("(kc p) (nt c) -> nt p kc c", p=128, c=512)
            lst = []
            for nt in range(nt_n):
                for gi, k0 in enumerate(range(0, kct, 16)):
                    k1 = min(kct, k0 + 16)
                    lst.append((dst[nt, :, k0:k1, :], src[nt, :, k0:k1, :], r_wb[name][nt][gi]))
            return lst

        def cast_emit(lst):
            for d_ap, s_ap, r in lst:
                POOL.dma(d_ap, s_ap, writes=[r])

        def cast_weight(name):
            cast_emit(cast_list(name))

        rest_casts = []
        for _n in ["na", "gqa", "mix", "ff1", "ff2", "gate", "proj"]:
            rest_casts.extend(cast_list(_n))
        N_OWN_TILES = cfg.TOK_OWN // T
        CASTS_PER_TILE = (len(rest_casts) + N_OWN_TILES - 2) // (N_OWN_TILES - 1)

        barrier()
        r_const.w = None

        def tiles_for(kindset):
            return [nt for nt in range(NT_IN) if any(cfg.kinds[nt * 4 + c][0] in kindset for c in range(4))]

        ALLK = {"naq", "nak", "nav", "gq", "gk", "gv", "ga", "gb"}
        p1_tiles = []
        for i in range(SEQ // T):
            p1_tiles.append(dict(x0=i * T, kinds=ALLK, own=i * T, na=i * T, gk=i * T))
        for i in range(HALF // T):
            p1_tiles.append(dict(x0=SEQ + i * T, kinds=ALLK, own=SEQ + i * T, na=SEQ + i * T, gk=SEQ + i * T))
        for i in range(HALF // T):
            p1_tiles.append(dict(x0=SEQ + HALF + i * T, kinds={"gk", "gv"}, own=None, na=None,
                                 gk=SEQ + HALF + i * T))
        p1_tiles.append(dict(x0=SEQ + 2 * HALF, kinds={"nak", "nav"}, own=None, na=SEQ + HALF, gk=None))
        for tl in p1_tiles:
            tl["wtiles"] = tiles_for(tl["kinds"])
        p1_tiles = [tl for tl in p1_tiles if tl["own"] is None] + [tl for tl in p1_tiles if tl["own"] is not None]
        first_need = []
        for tl in p1_tiles:
            for nt in tl["wtiles"]:
                if nt not in first_need:
                    first_need.append(nt)

        _cl = cast_list("in")
        assert len(_cl) == NT_IN
        cast_emit([_cl[nt] for nt in first_need])

        wseq = []
        for tl in p1_tiles:
            for nt in tl["wtiles"]:
                wseq.append((wb["in"][nt], KC, r_wb["in"][nt]))
        for t in range(cfg.TOK_OWN // T):
            for nt in range(NT_D):
                wseq.append((wb["na"][nt], NAKC, r_wb["na"][nt], wb["gqa"][nt], GQKC, r_wb["gqa"][nt]))
            for nt in range(NT_D):
                wseq.append((wb["mix"][nt], KC, r_wb["mix"][nt]))
            for half in range(2):
                for nt in range(FF_HALF // 512):
                    i = half * (FF_HALF // 512) + nt
                    wseq.append((wb["ff1"][i], KC, r_wb["ff1"][i]))
                for n in range(NT_D):
                    for kg in range(KG):
                        kk = (half * KG + kg) * KGS
                        wseq.append((wb["ff2"][n, :, kk:kk + KGS, :], KGS, [r_wb["ff2"][n][kk // 16]]))
            for nt in range(NT_D):
                wseq.append((wb["gate"][nt], KC, r_wb["gate"][nt]))
            wseq.append((wb["proj"].rearrange("n p k c -> p n k c"), NT_D * PKC,
                         [r for nt in range(NT_D) for r in r_wb["proj"][nt]]))
        wstate = {"emitted": 0, "taken": 0}

        def wget(held_prev=0):
            i = wstate["taken"]
            lim = min(len(wseq), i + NWS - held_prev)
            while wstate["emitted"] < lim:
                j = wstate["emitted"]
                ent = wseq[j]
                src, kc, rs = ent[0], ent[1], ent[2]
                sl = j % NWS
                SP._sync([], [r_wring[sl], r_wring2[sl]])
                dst = wring[sl][:, 0:kc, :]
                if len(src.shape) == 4:
                    dst = dst.rearrange("p (n k) c -> p n k c", k=src.shape[2])
                SP.dma(dst, src, reads=rs, writes=[r_wring[sl]])
                if len(ent) == 6:
                    SP.dma(wring[sl][:, kc:kc + ent[4], :], ent[3], reads=ent[5], writes=[r_wring2[sl]])
                wstate["emitted"] += 1
            wstate["taken"] += 1
            if len(wseq[i]) == 6:
                return wring[i % NWS], r_wring[i % NWS], r_wring2[i % NWS]
            return wring[i % NWS], r_wring[i % NWS]

        cp_rr = [0]

        def evac_copy(out_ap, in_ap, reads, writes):
            cp_rr[0] += 1
            if cp_rr[0] % 2:
                DVE.op(lambda e: e.tensor_copy(out=out_ap, in_=in_ap), reads=reads, writes=writes)
            else:
                ACT.op(lambda e: e.activation(out=out_ap, in_=in_ap, func=AF.Copy), reads=reads, writes=writes)

        def norm_scale(src, r_src, xn_, r_xn_, sq_split=True):
            grp = [list(r) if isinstance(r, (list, tuple)) else [r] for r in r_xn_]
            for s in range(4):
                if s < 2 or not sq_split:
                    ACT.op(lambda e, s=s: e.activation(out=xn_[:, s, :], in_=src[:, s, :], func=AF.Square,
                                                       scale=float(D) ** -0.5, accum_out=stat[:, s:s + 1]),
                           reads=[r_src[s]], writes=grp[s] + [r_stat_a[s]])
                else:
                    DVE.op(lambda e, s=s: e.scalar_tensor_tensor(out=xn_[:, s, :], in0=src[:, s, :], scalar=1.0 / D,
                                                                 in1=src[:, s, :], op0=ALU.mult, op1=ALU.mult,
                                                                 accum_out=stat[:, s:s + 1]),
                           reads=[r_src[s]], writes=grp[s] + [r_stat_a[s]])
            ACT.op(lambda e: e.activation(out=stat[:, 0:4], in_=stat[:, 0:4], func=AF.Sqrt, bias=EPS),
                   reads=r_stat_a, writes=[r_stat])
            DVE.op(lambda e: e.reciprocal(out=stat[:, 4:8], in_=stat[:, 0:4]), reads=[r_stat], writes=[r_stat])
            for s in (0, 1):
                ACT.op(lambda e, s=s: e.activation(out=xn_[:, s, :], in_=src[:, s, :], func=AF.Copy,
                                                   scale=stat[:, 4 + s:5 + s]),
                       reads=[r_src[s], r_stat], writes=grp[s])
            for s in (2, 3):
                DVE.op(lambda e, s=s: e.tensor_scalar(out=xn_[:, s, :], in0=src[:, s, :],
                                                      scalar1=stat[:, 4 + s:5 + s], scalar2=None, op0=ALU.mult),
                       reads=[r_src[s], r_stat], writes=grp[s])

        def transpose_evac(xn_, r_xn_, hT_, r_hT_, gain):
            grp = [list(r) if isinstance(r, (list, tuple)) else [r] for r in r_xn_]
            for k in range(KC):
                pb, rb = bank()
                pbb = pb[:].bitcast(BF16)
                for s in range(4):
                    PE.op(lambda e, s=s, k=k, pbb=pbb: e.transpose(out=pbb[:, s * 128:(s + 1) * 128],
                                                                  in_=xn_[:, s, k * 128:(k + 1) * 128],
                                                                  identity=ident[:]),
                          reads=grp[s], writes=[rb], inc=(s == 3))
                if k % 2 == 0:
                    DVE.op(lambda e, k=k, pbb=pbb: e.tensor_scalar(out=hT_[:, k, :], in0=pbb[:, 0:512],
                                                                  scalar1=gain[:, k:k + 1], scalar2=None,
                                                                  op0=ALU.mult),
                           reads=[rb], writes=[r_hT_[k]])
                else:
                    ACT.op(lambda e, k=k, pbb=pbb: e.activation(out=hT_[:, k, :], in_=pbb[:, 0:512], func=AF.Copy,
                                                               scale=gain[:, k:k + 1]),
                           reads=[rb], writes=[r_hT_[k]])

        def norm_transpose(src, r_src, xn_, r_xn_, hT_, r_hT_, gain, sq_split=True):
            norm_scale(src, r_src, xn_, r_xn_, sq_split)
            transpose_evac(xn_, r_xn_, hT_, r_hT_, gain)

        cv = Carver()
        xt = cv.take([4, D], F32)
        r_xt = [Res(f"xt{i}") for i in range(4)]
        xn = cv.take([4, D], BF16)
        r_xn = [Res(f"xn{s}") for s in range(4)]
        hT_b = [cv.take([KC, 512], BF16) for _ in range(2)]
        r_hT_b = [[Res(f"hT{i}_{k}") for k in range(KC)] for i in range(2)]
        NSTG = 8
        stg = [cv.take([4, 512], BF16) for _ in range(NSTG)]
        r_stg = [[Res(f"stg{i}_{c}") for c in range(4)] for i in range(NSTG)]
        cst = cv.take([2, 512], F32)
        r_cst = Res("cst")
        t_sq = [cv.take([512], F32) for _ in range(2)]
        t_gq = [cv.take([512], F32) for _ in range(2)]
        t_rs = [cv.take([512], F32) for _ in range(2)]
        t_t1 = [cv.take([512], F32) for _ in range(2)]
        r_tq = [[Res(f"tq{i}_{j}") for j in range(4)] for i in range(2)]
        stg_rr = [0]
        rope_rr = [0]

        scr = {"naq": naqT, "nak": nakT, "gq": gqT, "gk": gkT, "ga": sgaT, "gb": sgbT}
        FEAT = {"naq", "nak", "gq", "gk", "ga", "gb"}

        def dst_off(tl, kind):
            if kind in ("naq", "gq", "ga", "gb"):
                return tl["own"]
            if kind in ("nak", "nav"):
                return tl["na"]
            return tl["gk"]

        rope_pend = []

        def rope_chunk(pb, rb, gain_ap, out_ap, r_out):
            i = rope_rr[0] % 2
            rope_rr[0] += 1
            sq, gq_ = t_sq[i], t_gq[i]
            rq = r_tq[i]
            ACT.op(lambda e: e.activation(out=sq, in_=pb[:], func=AF.Square), reads=[rb], writes=[rq[0]])
            ACT.op(lambda e: e.activation(out=gq_, in_=pb[:], func=AF.Copy, scale=gain_ap),
                   reads=[rb], writes=[rq[1]])
            rope_pend.append((i, out_ap, r_out))

        def rope_flush(keep=0):
            while len(rope_pend) > keep:
                i, out_ap, r_out = rope_pend.pop(0)
                sq, gq_, rs, t1 = t_sq[i], t_gq[i], t_rs[i], t_t1[i]
                rq = r_tq[i]
                p_ss, r_ss = bank()
                PE.op(lambda e, p_ss=p_ss, sq=sq: e.matmul(p_ss[:], lhsT=onesf[:], rhs=sq, start=True, stop=True),
                      reads=[rq[0]], writes=[r_ss])
                p_rt, r_rt = bank()
                PE.op(lambda e, p_rt=p_rt, gq_=gq_: e.matmul(p_rt[:], lhsT=permT[:], rhs=gq_, start=True, stop=True),
                      reads=[rq[1]], writes=[r_rt])
                ACT.op(lambda e, rs=rs, p_ss=p_ss: e.activation(out=rs, in_=p_ss[:], func=AF.Sqrt, bias=EPS),
                       reads=[r_ss], writes=[rq[2]])
                DVE.op(lambda e, rs=rs: e.reciprocal(out=rs, in_=rs), reads=[rq[2]], writes=[rq[2]])
                DVE.op(lambda e, t1=t1, p_rt=p_rt: e.tensor_tensor(out=t1, in0=p_rt[:], in1=cst[:, 1, :], op=ALU.mult),
                       reads=[r_rt, r_cst], writes=[rq[3]])
                DVE.op(lambda e, gq_=gq_: e.tensor_tensor(out=gq_, in0=gq_, in1=cst[:, 0, :], op=ALU.mult),
                       reads=[rq[1], r_cst], writes=[rq[1]])
                DVE.op(lambda e, t1=t1, gq_=gq_: e.tensor_tensor(out=t1, in0=t1, in1=gq_, op=ALU.add),
                       reads=[rq[3], rq[1]], writes=[rq[3]])
                DVE.op(lambda e, t1=t1, rs=rs, out_ap=out_ap: e.tensor_tensor(out=out_ap, in0=t1, in1=rs, op=ALU.mult),
                       reads=[rq[3], rq[2]], writes=[r_out])

        def p1_xload(tl, q=None):
            x0 = tl["x0"]
            (q or SP).dma(xt, xall[x0:x0 + T, :].rearrange("(s p) d -> p s d", p=128), writes=r_xt)

        def phase1_tile(tl, next_tl, next2_tl, hT, r_hT):
            if tl["kinds"] & {"gq", "gk"}:
                g0 = tl["gk"]
                SP.dma(cst, ropecs[:, :, g0:g0 + T].rearrange("c p t -> p c t"), writes=[r_cst])
            transpose_evac(xn, r_xn, hT, r_hT, gsb["ln_mix_pre"])
            mid = (len(tl["wtiles"]) * 2) // 3
            for wi_, nt in enumerate(tl["wtiles"]):
                if wi_ == mid and next_tl is not None:
                    norm_scale(xt, r_xt, xn, r_xn)
                    if next2_tl is not None:
                        p1_xload(next2_tl, POOL if tl["own"] is not None else SP)
                wt, r_wt = wget()
                if tl["own"] is not None and rest_casts and wi_ % 3 == 2:
                    cast_emit(rest_casts[:1])
                    del rest_casts[:1]
                runs = []
                for c in range(4):
                    kind, idx = cfg.kinds[nt * 4 + c]
                    if kind not in tl["kinds"]:
                        continue
                    if runs and runs[-1][0] == kind and runs[-1][2] + runs[-1][3] == c:
                        runs[-1][3] += 1
                    else:
                        runs.append([kind, idx, c, 1])
                for (kind, idx0, c0, n) in runs:
                    off = dst_off(tl, kind)
                    si = stg_rr[0] % NSTG
                    stg_rr[0] += 1
                    sg, r_sg = stg[si], r_stg[si]
                    if kind in FEAT:
                        for cc in range(n):
                            c = c0 + cc
                            pb, rb = bank()
                            for k in range(KC):
                                PE.op(lambda e, k=k, c=c, pb=pb, wt=wt: e.matmul(
                                    pb[:], lhsT=wt[:, k, c * 128:(c + 1) * 128], rhs=hT[:, k, :],
                                    start=(k == 0), stop=(k == KC - 1)),
                                    reads=[r_wt, r_hT[k]], writes=[rb], inc=(k == KC - 1))
                            o_ap = sg[:, c, :]
                            if kind in ("naq", "nak"):
                                evac_copy(o_ap, pb[:], [rb], [r_sg[c]])
                            elif kind in ("ga", "gb"):
                                ACT.op(lambda e, o_ap=o_ap, pb=pb: e.activation(out=o_ap, in_=pb[:],
                                                                                 func=AF.Sigmoid),
                                       reads=[rb], writes=[r_sg[c]])
                            else:
                                rope_flush(keep=0)
                                rope_chunk(pb, rb, (qn if kind == "gq" else kn)[:, 0:1], o_ap, r_sg[c])
                        rope_flush()
                        POOL.dma(scr[kind][idx0:idx0 + n, :, off:off + T].rearrange("h p t -> p h t"),
                                 sg[:, c0:c0 + n, :], reads=r_sg[c0:c0 + n])
                    else:
                        dst = nav if kind == "nav" else gv
                        for s in range(4):
                            pb, rb = bank()
                            for k in range(KC):
                                PE.op(lambda e, k=k, s=s, pb=pb, wt=wt, c0=c0, n=n: e.matmul(
                                    pb[:, 0:n * 128], lhsT=hT[:, k, s * 128:(s + 1) * 128],
                                    rhs=wt[:, k, c0 * 128:(c0 + n) * 128], start=(k == 0), stop=(k == KC - 1)),
                                    reads=[r_wt, r_hT[k]], writes=[rb], inc=(k == KC - 1))
                            evac_copy(sg[:, s, c0 * 128:(c0 + n) * 128], pb[:, 0:n * 128], [rb], [r_sg[s]])
                        POOL.dma(dst[off:off + T, idx0 * 128:(idx0 + n) * 128].rearrange("(s p) c -> p s c", p=128),
                                 sg[:, :, c0 * 128:(c0 + n) * 128], reads=r_sg)

        p1_xload(p1_tiles[0])
        norm_scale(xt, r_xt, xn, r_xn)
        p1_xload(p1_tiles[1])
        for ti, tl in enumerate(p1_tiles):
            phase1_tile(tl, p1_tiles[ti + 1] if ti + 1 < len(p1_tiles) else None,
                        p1_tiles[ti + 2] if ti + 2 < len(p1_tiles) else None, hT_b[ti % 2], r_hT_b[ti % 2])
        cast_emit(rest_casts)
        del rest_casts[:]

        barrier()

        cv = Carver()
        m3r = cv.take([NAH, NSLOT * 64], BF16)
        r_m3r = Res("m3r")
        mk = cv.take([NSLOT * 64], BF16)
        r_mk = Res("mk")
        rtmp = cv.take([NSLOT * 64], F32)
        r_rtmp = Res("rtmp")
        A16 = cv.take([8, 128], BF16)
        rowB = cv.take([12, 512], BF16)
        r_c2 = Res("c2")
        r_c2a = Res("c2a")
        NWIN = 2
        kwin = [cv.take([HG, 1024], BF16) for _ in range(NWIN)]
        vwin = [cv.take([8, HG * 128], BF16) for _ in range(NWIN)]
        qwin = [cv.take([HG, 512], BF16) for _ in range(NWIN)]
        r_kw = [[Res(f"kw{i}_{j}") for j in range(3)] for i in range(NWIN)]
        r_vw = [[Res(f"vw{i}_{j}") for j in range(3)] for i in range(NWIN)]
        r_qw = [Res(f"qw{i}") for i in range(NWIN)]
        ostg = [cv.take([max(NAH, GQH), 512], BF16) for _ in range(2)]
        r_ostg = [Res(f"ostg{i}") for i in range(2)]
        NPT = 4
        est = [cv.take([512], BF16) for _ in range(NPT)]
        r_est = [Res(f"est{i}") for i in range(NPT)]
        ptt = [cv.take([512], BF16) for _ in range(NPT)]
        r_ptt = [Res(f"ptt{i}") for i in range(NPT)]
        rec = [cv.take([512], F32) for _ in range(2)]
        r_rec = [Res(f"rec{i}") for i in range(2)]

        SP.dma(A16, A16_d, writes=[r_c2a])
        SP.dma(rowB, rowB_d, writes=[r_c2])
        na_blocks = []
        for b in range(8):
            R0 = 8 * b
            v0, v1 = R0 - 4, R0 + 12
            lo, hi = max(v0, 0), min(v1, GRID)
            pieces = [(lo * GRID, (hi - lo) * GRID, (lo - v0) * GRID)]
            chunks = [j for j in range(8) if (v0 + 2 * j) >= 0 and (v0 + 2 * j + 1) < GRID]
            na_blocks.append(dict(q0=R0 * GRID, pieces=pieces, chunks=chunks, bidx=b))
        for b in range(4):
            v0 = 8 * b
            pieces = []
            for vr0, vr1 in [(0, 4), (4, 36), (36, 40)]:
                lo, hi = max(v0, vr0), min(v0 + 16, vr1)
                if lo < hi:
                    if vr0 == 0:
                        src = SEQ + HALF + lo * GRID
                    elif vr0 == 4:
                        src = SEQ + (lo - 4) * GRID
                    else:
                        src = SEQ + HALF + 256 + (lo - 36) * GRID
                    pieces.append((src, (hi - lo) * GRID, (lo - v0) * GRID))
            na_blocks.append(dict(q0=SEQ + 8 * b * GRID, pieces=pieces, chunks=list(range(8)), bidx=8 + b))
        na_units = [(blk, hg) for blk in na_blocks for hg in range(NHG)]

        def na_load(unit, wi):
            blk, hg = unit
            h0 = hg * HG
            for pi, (src, n, dof) in enumerate(blk["pieces"]):
                SP.dma(kwin[wi][:, :, dof:dof + n], nakT[h0:h0 + HG, :, src:src + n].rearrange("h p t -> p h t"),
                       writes=[r_kw[wi][pi]])
                SP.dma(vwin[wi][:, dof // 128:(dof + n) // 128, :],
                       nav[src:src + n, h0 * 128:(h0 + HG) * 128].rearrange("(j p) c -> p j c", p=128),
                       writes=[r_vw[wi][pi]])
            q0 = blk["q0"]
            SP.dma(qwin[wi], naqT[h0:h0 + HG, :, q0:q0 + 512].rearrange("h p t -> p h t"), writes=[r_qw[wi]])

        pt_rr = [0]
        rec_rr = [0]
        ostg_rr = [0]
        acc_rr = [0]
        bank_mod[0], bank_mod[1] = 4, 4

        LOOK = 3
        apend = []

        accs = {}

        def attn_group(nchunks, r_v, out_ap, r_out, view=None, done_cb=None, den="pe"):
            par = acc_rr[0] % 2
            a = par * 2
            acc_rr[0] += 1
            g = dict(n=nchunks, p_o=ps[a], r_o=r_ps[a], p_d=ps[a + 1], r_d=r_ps[a + 1], r_v=r_v, out_ap=out_ap,
                     r_out=r_out, view=view, done_cb=done_cb, den=den)
            if den == "vec":
                g["acc"] = accs["ap"][par]
                g["r_acc"] = accs["res"][par]
                g["dc"] = 0
                g["init"] = set()
            return g

        def attn_pv(step):
            g, idx, i, lhsT_ap, pt_ap, r_pt = step
            p_o, r_o, p_d, r_d, n = g["p_o"], g["r_o"], g["p_d"], g["r_d"], g["n"]
            if g["den"] == "pe":
                PE.op(lambda e: e.matmul(p_o[:], lhsT=lhsT_ap, rhs=pt_ap, start=(idx == 0), stop=(idx == n - 1)),
                      reads=g["r_v"] + [r_pt], writes=[r_o], inc=False)
                PE.op(lambda e: e.matmul(p_d[:], lhsT=onesb[:], rhs=pt_ap, start=(idx == 0), stop=(idx == n - 1)),
                      reads=[r_pt], writes=[r_d, r_o], inc=True)
            else:
                on_pe = (idx % 2 == 1)
                PE.op(lambda e: e.matmul(p_o[:], lhsT=lhsT_ap, rhs=pt_ap, start=(idx == 0), stop=(idx == n - 1)),
                      reads=g["r_v"] + [r_pt], writes=[r_o], inc=not on_pe)
                if on_pe:
                    PE.op(lambda e: e.matmul(p_d[:], lhsT=onesb[:], rhs=pt_ap, start=(idx == 1), stop=False),
                          reads=[r_pt], writes=[r_d, r_o], inc=True)
                else:
                    if idx % 8 == 6:
                        ai, eng = 2, POOL
                    else:
                        ai, eng = g["dc"] % 2, DVE
                        g["dc"] += 1
                    acc_ap, r_acc = g["acc"][ai], g["r_acc"][ai]
                    first_use = ai not in g["init"]
                    g["init"].add(ai)
                    if first_use:
                        eng.op(lambda e: e.tensor_copy(out=acc_ap, in_=pt_ap), reads=[r_pt], writes=[r_acc])
                    else:
                        eng.op(lambda e: e.tensor_tensor(out=acc_ap, in0=acc_ap, in1=pt_ap, op=ALU.add),
                               reads=[r_pt, r_acc], writes=[r_acc])
                if idx == n - 1:
                    assert g["init"] == {0, 1, 2} and n > 1
                    for ai2 in range(3):
                        a_ap = g["acc"][ai2]
                        PE.op(lambda e, a_ap=a_ap, ai2=ai2: e.matmul(p_d[:], lhsT=ones1[:], rhs=a_ap,
                                                                     start=False, stop=(ai2 == 2)),
                              reads=[g["r_acc"][ai2]], writes=[r_d], inc=(ai2 == 2))
            if idx == n - 1:
                ri = rec_rr[0] % 2
                rec_rr[0] += 1
                rec_ap, r_rc = rec[ri], r_rec[ri]
                out_ap, view = g["out_ap"], g["view"]
                ACT.op(lambda e: e.activation(out=rec_ap, in_=p_d[:], func=AF.Ln), reads=[r_d], writes=[r_rc])
                ACT.op(lambda e: e.activation(out=rec_ap, in_=rec_ap, func=AF.Exp, scale=-1.0), reads=[r_rc],
                       writes=[r_rc])
                if view is None:
                    DVE.op(lambda e: e.tensor_tensor(out=out_ap, in0=p_o[:], in1=rec_ap, op=ALU.mult),
                           reads=[r_o, r_rc], writes=[g["r_out"]])
                else:
                    DVE.op(lambda e: e.tensor_tensor(out=out_ap, in0=view(p_o[:]), in1=view(rec_ap), op=ALU.mult),
                           reads=[r_o, r_rc], writes=[g["r_out"]])
                if g["done_cb"] is not None:
                    g["done_cb"]()

        def attn_step(g, idx, qk_fn, post_fn, lhsT_ap):
            p_s, r_s = bank()
            qk_fn(idx, p_s, r_s)
            i = pt_rr[0] % NPT
            pt_rr[0] += 1
            post_fn(idx, p_s, r_s, i)
            apend.append((g, idx, i, lhsT_ap, ptt[i], r_ptt[i]))
            while len(apend) > LOOK:
                attn_pv(apend.pop(0))

        def attn_flush():
            while apend:
                attn_pv(apend.pop(0))

        na_load(na_units[0], 0)
        SP.dma(mk, maskK, writes=[r_mk])
        for h in range(NAH):
            SP.dma(rtmp, rpbG[h], writes=[r_rtmp])
            ACT.op(lambda e: e.activation(out=rtmp, in_=rtmp, func=AF.Exp), reads=[r_rtmp], writes=[r_rtmp])
            DVE.op(lambda e, h=h: e.tensor_tensor(out=m3r[:, h, :], in0=rtmp, in1=mk, op=ALU.mult),
                   reads=[r_rtmp, r_mk], writes=[r_m3r])

        for ui, (blk, hg) in enumerate(na_units):
            wi = ui % NWIN
            if hg == 0:
                oi = ostg_rr[0] % 2
                ostg_rr[0] += 1
            chunks = blk["chunks"]
            r_kq = r_kw[wi] + [r_qw[wi]]
            for hl in range(HG):
                h = hg * HG + hl

                def qk_fn(jx, p_s, r_s, hl=hl, wi=wi, blk=blk, chunks=chunks, r_kq=r_kq):
                    j = chunks[jx]
                    PE.op(lambda e: e.matmul(p_s[:], lhsT=kwin[wi][:, hl, j * 128:(j + 1) * 128],
                                             rhs=qwin[wi][:, hl, :], start=True, stop=False),
                          reads=r_kq, writes=[r_s], inc=False)
                    PE.op(lambda e: e.matmul(p_s[:], lhsT=A16[:, j, :], rhs=rowB[:, blk["bidx"], :],
                                             start=False, stop=True),
                          reads=[r_c2, r_c2a], writes=[r_s], inc=True)

                def post_fn(jx, p_s, r_s, i, h=h, chunks=chunks):
                    j = chunks[jx]
                    es_ap, pt_ap = est[i], ptt[i]
                    ACT.op(lambda e: e.activation(out=es_ap, in_=p_s[:], func=AF.Exp, scale=SCALE),
                           reads=[r_s], writes=[r_est[i]])
                    c0 = (14 - 2 * j) * 64
                    DVE.op(lambda e: e.tensor_tensor(out=pt_ap, in0=es_ap, in1=m3r[:, h, c0:c0 + 512],
                                                     op=ALU.mult),
                           reads=[r_est[i], r_m3r], writes=[r_ptt[i]])

                cb = None
                if hg == NHG - 1 and hl == HG - 1:
                    def cb(q0=blk["q0"], oi=oi):
                        POOL.dma(aT_d[:, :, q0:q0 + 512].rearrange("h p t -> p h t"), ostg[oi][:, 0:NAH, :],
                                 reads=[r_ostg[oi]])
                g = attn_group(len(chunks), r_vw[wi], ostg[oi][:, h, :], r_ostg[oi], done_cb=cb)
                for jx in range(len(chunks)):
                    attn_step(g, jx, qk_fn, post_fn, vwin[wi][:, chunks[jx], hl * 128:(hl + 1) * 128])
                if hl == 0 and ui + 1 < len(na_units):
                    na_load(na_units[ui + 1], (ui + 1) % NWIN)
        attn_flush()

        barrier()

        cv = Carver()
        ostg = [cv.take([GQH, 512], BF16) for _ in range(2)]
        r_ostg = [Res(f"ostgb{i}") for i in range(2)]
        NPT = 8
        ptt = [cv.take([512], BF16) for _ in range(NPT)]
        r_ptt = [Res(f"pttb{i}") for i in range(NPT)]
        rec = [cv.take([512], F32) for _ in range(2)]
        r_rec = [Res(f"recb{i}") for i in range(2)]
        gkT_sb = cv.take([KVH, SEQ], BF16)
        gv_sb = cv.take([SEQ // 128, cfg.KVW], BF16)
        r_gk, r_gv = Res("gk"), Res("gv")
        accs["ap"] = [[cv.take([512], F32) for _ in range(3)] for _ in range(2)]
        accs["res"] = [[Res(f"acc{p}_{i}") for i in range(3)] for p in range(2)]
        NQT = 2
        gqt = [cv.take([GQH * 128], BF16) for _ in range(NQT)]
        r_gqt = [Res(f"gqt{i}") for i in range(NQT)]

        def gview(ap):
            return ap.rearrange("p (g t) -> p g t", t=128)

        for part in range(2):
            kv0 = 0 if part == 0 else SEQ
            q00 = 0 if part == 0 else SEQ
            nqb = (SEQ if part == 0 else HALF) // 128
            SP.dma(gkT_sb, gkT[:, :, kv0:kv0 + SEQ].rearrange("h p t -> p h t"), writes=[r_gk])
            SP.dma(gv_sb, gv[kv0:kv0 + SEQ, :].rearrange("(j p) c -> p j c", p=128), writes=[r_gv])

            def q_load(qb, q00=q00):
                qi = qb % NQT
                q0 = q00 + qb * 128
                SP.dma(gview(gqt[qi]), gqT[:, :, q0:q0 + 128].rearrange("h p t -> p h t"), writes=[r_gqt[qi]])

            q_load(0)
            for qb in range(nqb):
                if qb + 1 < nqb:
                    q_load(qb + 1)
                qi = qb % NQT
                if qb % 4 == 0:
                    oi = ostg_rr[0] % 2
                    ostg_rr[0] += 1
                for kvh in range(KVH):
                    def qk_fn(j, p_s, r_s, kvh=kvh, qi=qi):
                        PE.op(lambda e: e.matmul(p_s[:], lhsT=gkT_sb[:, kvh, j * 128:(j + 1) * 128],
                                                 rhs=gqt[qi][:, kvh * 512:(kvh + 1) * 512], start=True, stop=True),
                              reads=[r_gk, r_gqt[qi]], writes=[r_s], inc=True)

                    def post_fn(j, p_s, r_s, i):
                        pt_ap = ptt[i]
                        ACT.op(lambda e: e.activation(out=pt_ap, in_=p_s[:], func=AF.Exp, scale=SCALE),
                               reads=[r_s], writes=[r_ptt[i]])

                    cb = None
                    if qb % 4 == 3 and kvh == KVH - 1:
                        def cb(q0=q00 + (qb - 3) * 128, oi=oi):
                            POOL.dma(bT_d[:, :, q0:q0 + 512].rearrange("h p t -> p h t"), ostg[oi][:, 0:GQH, :],
                                     reads=[r_ostg[oi]])
                    o_ap = ostg[oi][:, kvh * G:(kvh + 1) * G, (qb % 4) * 128:(qb % 4 + 1) * 128]
                    g = attn_group(SEQ // 128, [r_gv], o_ap, r_ostg[oi], view=gview, done_cb=cb, den="vec")
                    for j in range(SEQ // 128):
                        attn_step(g, j, qk_fn, post_fn, gv_sb[:, j, kvh * 128:(kvh + 1) * 128])
            attn_flush()

        barrier()
        bank_mod[0], bank_mod[1] = 8, 0

        cv = Carver()
        xt = cv.take([4, D], F32)
        r_xt = [Res(f"xt3_{i}") for i in range(4)]
        yb = cv.take([4, D], F32)
        r_yb = [[Res(f"yb{s}_{n}") for n in range(NT_D)] for s in range(4)]
        ssqp = cv.take([4 * NT_D], F32)
        r_ssqp = [Res(f"ssqp{i}") for i in range(4)]
        jnk = [cv.take([512], BF16) for _ in range(2)]
        r_jnk = [Res(f"jnk{i}") for i in range(2)]
        jnk_rr = [0]
        hT = cv.take([KC, 512], BF16)
        r_hT = [Res(f"hT3_{k}") for k in range(KC)]
        NARN = 16
        arn_bytes = max(UCH * 1024, 4 * D * 2, (NAKC + GQKC) * 1024)
        arn = cv.take([arn_bytes // 2], BF16)
        r_arn = [Res(f"arn{i}") for i in range(NARN)]
        seg = arn_bytes // NARN

        def arn_res(b0, b1):
            return r_arn[b0 // seg:(b1 - 1) // seg + 1]

        uT = arn[:, 0:UCH * 512].rearrange("p (k t) -> p k t", t=512)
        xn3 = arn[:, 0:4 * D].rearrange("p (s d) -> p s d", d=D)
        aT_sb = arn[:, 0:NAKC * 512].rearrange("p (k t) -> p k t", t=512)
        bT_sb = arn[:, NAKC * 512:(NAKC + GQKC) * 512].rearrange("p (k t) -> p k t", t=512)
        gbc = cv.take([D], F32)
        r_gbc1 = Res("gbc1")
        r_uT = [arn_res(k * 1024, (k + 1) * 1024) for k in range(UCH)]
        r_xn3 = [arn_res(s * D * 2, (s + 1) * D * 2) for s in range(4)]
        r_aT = arn_res(0, NAKC * 1024)
        r_bT = arn_res(NAKC * 1024, (NAKC + GQKC) * 1024)
        ptile = cv.take([4, PLE], F32)
        r_ptile = Res("ptile")
        pbf = cv.take([4, PLE], BF16)
        r_pbf = Res("pbf")
        pT = cv.take([PKC, 512], BF16)
        r_pT = Res("pT")
        NGB = 6
        gbuf = [cv.take([2, 512], BF16) for _ in range(NGB)]
        r_gbuf = [[Res(f"gbuf{i}_{j}") for j in range(2)] for i in range(NGB)]
        NTM = 2
        tm1 = [cv.take([512], F32) for _ in range(NTM)]
        tm2 = [cv.take([512], F32) for _ in range(NTM)]
        r_tm1 = [Res(f"tm1_{i}") for i in range(NTM)]
        r_tm2 = [Res(f"tm2_{i}") for i in range(NTM)]
        rl = [cv.take([512], BF16) for _ in range(2)]
        r_rl = [Res(f"rl{i}") for i in range(2)]
        gb_rr = [0]
        tm_rr = [0]
        rl_rr = [0]

        def tokmajor_linear(actT, r_act_fn, ktiles, evac, shared_w=None):
            for n in range(NT_D):
                banks = [bank() for _ in range(4)]
                for kt, (kbase, kcount) in enumerate(ktiles):
                    if shared_w is None:
                        wt, r_wt = wget()
                    else:
                        wt, r_wt = shared_w[0][:, n * kcount:(n + 1) * kcount, :], shared_w[1]
                    for k in range(kcount):
                        for s in range(4):
                            pb, rb = banks[s]
                            first = (kt == 0 and k == 0)
                            last = (kt == len(ktiles) - 1 and k == kcount - 1)
                            PE.op(lambda e, k=k, s=s, pb=pb, wt=wt, kk=kbase + k, first=first, last=last: e.matmul(
                                pb[:], lhsT=actT[:, kk, s * 128:(s + 1) * 128], rhs=wt[:, k, :], start=first, stop=last),
                                reads=[r_wt] + r_act_fn(kbase + k), writes=[rb],
                                inc=(last or (k == kcount - 1 and s == 3)))
                for s in range(4):
                    evac(s, n, banks[s][0], banks[s][1])

        def gain_load(gain_name):
            SP.dma(gbc, gains[gain_name].partition_broadcast(128), writes=[r_gbc1])

        def blk_post(s, n):
            ji = jnk_rr[0] % 2
            jnk_rr[0] += 1
            blk = yb[:, s, n * 512:(n + 1) * 512]
            ACT.op(lambda e: e.activation(out=jnk[ji], in_=blk, func=AF.Square, scale=float(D) ** -0.5,
                                          accum_out=ssqp[:, s * NT_D + n:s * NT_D + n + 1]),
                   reads=[r_yb[s][n]], writes=[r_jnk[ji], r_ssqp[s]])
            POOL.op(lambda e: e.tensor_tensor(out=blk, in0=blk, in1=gbc[:, n * 512:(n + 1) * 512], op=ALU.mult),
                    reads=[r_yb[s][n], r_gbc1], writes=[r_yb[s][n]])

        def post_norm():
            DVE.op(lambda e: e.tensor_reduce(out=stat[:, 8:12], in_=ssqp.rearrange("p (s n) -> p s n", n=NT_D),
                                             axis=mybir.AxisListType.X, op=ALU.add),
                   reads=r_ssqp, writes=[r_stat])
            ACT.op(lambda e: e.activation(out=stat[:, 8:12], in_=stat[:, 8:12], func=AF.Sqrt, bias=EPS),
                   reads=[r_stat], writes=[r_stat])
            DVE.op(lambda e: e.reciprocal(out=stat[:, 12:16], in_=stat[:, 8:12]), reads=[r_stat], writes=[r_stat])
            for s in range(4):
                eng = DVE
                eng.op(lambda e, s=s: e.scalar_tensor_tensor(out=xt[:, s, :], in0=yb[:, s, :],
                                                             scalar=stat[:, 12 + s:13 + s], in1=xt[:, s, :],
                                                             op0=ALU.mult, op1=ALU.add),
                       reads=r_yb[s] + [r_stat, r_xt[s]], writes=[r_xt[s]])

        def p3_attn_load(t):
            t0 = t * T
            SP.dma(aT_sb, aT_d[:, :, t0:t0 + T].rearrange("h p t -> p h t"), writes=r_aT)
            SP.dma(bT_sb, bT_d[:, :, t0:t0 + T].rearrange("h p t -> p h t"), writes=r_bT)

        def p3_gate_load(t, m):
            t0 = t * T
            gi = m % NGB
            SP.dma(gbuf[gi][:, 0, :], sgaT[m, :, t0:t0 + T], writes=[r_gbuf[gi][0]])
            SP.dma(gbuf[gi][:, 1, :], sgbT[m, :, t0:t0 + T], writes=[r_gbuf[gi][1]])

        def phase3_tile(t):
            t0 = t * T
            if t == 0:
                p3_attn_load(0)
            for m0 in range(min(2, KC)):
                p3_gate_load(t, m0)
            for nt in range(NT_D):
                wa, r_wa, r_wg = wget()
                wg = wa[:, NAKC:NAKC + GQKC, :]
                for c in range(4):
                    m = nt * 4 + c
                    gi = m % NGB
                    if m + 2 < KC:
                        p3_gate_load(t, m + 2)
                    pa, ra = bank()
                    for k in range(NAKC):
                        PE.op(lambda e, k=k, c=c, pa=pa, wa=wa: e.matmul(
                            pa[:], lhsT=wa[:, k, c * 128:(c + 1) * 128], rhs=aT_sb[:, k, :], start=(k == 0),
                            stop=(k == NAKC - 1)), reads=[r_wa] + r_aT, writes=[ra], inc=(k == NAKC - 1))
                    pg, rg = bank()
                    for k in range(GQKC):
                        PE.op(lambda e, k=k, c=c, pg=pg, wg=wg: e.matmul(
                            pg[:], lhsT=wg[:, k, c * 128:(c + 1) * 128], rhs=bT_sb[:, k, :], start=(k == 0),
                            stop=(k == GQKC - 1)), reads=[r_wg] + r_bT, writes=[rg], inc=(k == GQKC - 1))
                    ti = tm_rr[0] % NTM
                    tm_rr[0] += 1
                    DVE.op(lambda e, pa=pa, gi=gi, ti=ti: e.tensor_tensor(out=tm1[ti], in0=pa[:],
                                                                         in1=gbuf[gi][:, 0, :], op=ALU.mult),
                           reads=[ra, r_gbuf[gi][0]], writes=[r_tm1[ti]])
                    DVE.op(lambda e, pg=pg, gi=gi, ti=ti: e.tensor_tensor(out=tm2[ti], in0=pg[:],
                                                                         in1=gbuf[gi][:, 1, :], op=ALU.mult),
                           reads=[rg, r_gbuf[gi][1]], writes=[r_tm2[ti]])
                    POOL.op(lambda e, m=m, ti=ti: e.tensor_tensor(out=hT[:, m, :], in0=tm1[ti], in1=tm2[ti],
                                                                  op=ALU.add),
                            reads=[r_tm1[ti], r_tm2[ti]], writes=[r_hT[m]])

            SP.dma(xt, xall[t0:t0 + T, :].rearrange("(s p) d -> p s d", p=128), writes=r_xt)
            SP.dma(ptile, pown[t0:t0 + T, :].rearrange("(s p) d -> p s d", p=128), writes=[r_ptile])

            def evac_y(s, n, pb, rb):
                evac_copy(yb[:, s, n * 512:(n + 1) * 512], pb[:], [rb], [r_yb[s][n]])
                blk_post(s, n)

            gain_load("ln_mix_post")
            tokmajor_linear(hT, lambda k: [r_hT[k]], [(0, KC)], evac_y)
            post_norm()
            norm_transpose(xt, r_xt, xn3, r_xn3, hT, r_hT, gsb["ln_mlp_pre"], sq_split=False)
            for half in range(2):
                for nt in range(FF_HALF // 512):
                    wt, r_wt = wget()
                    for c in range(4):
                        ff = nt * 4 + c
                        pb, rb = bank()
                        for k in range(KC):
                            PE.op(lambda e, k=k, c=c, pb=pb, wt=wt: e.matmul(
                                pb[:], lhsT=wt[:, k, c * 128:(c + 1) * 128], rhs=hT[:, k, :], start=(k == 0),
                                stop=(k == KC - 1)), reads=[r_wt, r_hT[k]], writes=[rb], inc=(k == KC - 1))
                        ri = rl_rr[0] % 2
                        rl_rr[0] += 1
                        ACT.op(lambda e, pb=pb, ri=ri: e.activation(out=rl[ri], in_=pb[:], func=AF.Relu),
                               reads=[rb], writes=[r_rl[ri]])
                        POOL.op(lambda e, ff=ff, ri=ri: e.tensor_tensor(out=uT[:, ff, :], in0=rl[ri],
                                                                        in1=rl[ri], op=ALU.mult),
                                reads=[r_rl[ri]], writes=r_uT[ff])

                def evac_f(s, n, pb, rb, half=half):
                    if half == 0:
                        evac_copy(yb[:, s, n * 512:(n + 1) * 512], pb[:], [rb], [r_yb[s][n]])
                    else:
                        DVE.op(lambda e: e.tensor_tensor(out=yb[:, s, n * 512:(n + 1) * 512], in0=pb[:],
                                                         in1=yb[:, s, n * 512:(n + 1) * 512], op=ALU.add),
                               reads=[rb, r_yb[s][n]], writes=[r_yb[s][n]])
                        blk_post(s, n)

                if half == 1:
                    gain_load("ln_mlp_post")
                tokmajor_linear(uT, lambda k: r_uT[k], [(kg * KGS, KGS) for kg in range(KG)], evac_f)
            post_norm()
            norm_transpose(xt, r_xt, xn3, r_xn3, hT, r_hT, gsb["ln_ple_pre"], sq_split=False)
            if t + 1 < cfg.TOK_OWN // T:
                p3_attn_load(t + 1)

            def evac_g(s, n, pb, rb):
                ACT.op(lambda e: e.activation(out=yb[:, s, n * 512:(n + 1) * 512], in_=pb[:], func=AF.Sigmoid),
                       reads=[rb], writes=[r_yb[s][n]])

            tokmajor_linear(hT, lambda k: [r_hT[k]], [(0, KC)], evac_g)
            ACT.op(lambda e: e.activation(out=pbf, in_=ptile, func=AF.Copy), reads=[r_ptile], writes=[r_pbf])
            for k in range(PKC):
                pb, rb = bank()
                pbb = pb[:].bitcast(BF16)
                for s in range(4):
                    PE.op(lambda e, s=s, k=k, pbb=pbb: e.transpose(out=pbb[:, s * 128:(s + 1) * 128],
                                                                  in_=pbf[:, s, k * 128:(k + 1) * 128],
                                                                  identity=ident[:]),
                          reads=[r_pbf], writes=[rb], inc=(s == 3))
                DVE.op(lambda e, k=k, pbb=pbb: e.tensor_copy(out=pT[:, k, :], in_=pbb[:, 0:512]), reads=[rb],
                       writes=[r_pT])

            def evac_e(s, n, pb, rb):
                DVE.op(lambda e: e.tensor_tensor(out=yb[:, s, n * 512:(n + 1) * 512], in0=pb[:],
                                                 in1=yb[:, s, n * 512:(n + 1) * 512], op=ALU.mult),
                       reads=[rb, r_yb[s][n]], writes=[r_yb[s][n]])
                blk_post(s, n)

            gain_load("ln_ple_post")
            tokmajor_linear(pT, lambda k: [r_pT], [(0, PKC)], evac_e, shared_w=wget())
            post_norm()
            POOL.dma(out[t0:t0 + T, :].rearrange("(s p) d -> p s d", p=128), xt, reads=r_xt)

        for t in range(cfg.TOK_OWN // T):
            phase3_tile(t)

        assert wstate["taken"] == len(wseq), (wstate, len(wseq))
        for P in POOL.dma_prods:
            if P.count > 0:
                POOL._need(P, P.count)
        build_program.stats = {e.name: (len(e.ops), e.prod.count) for e in engines}

        with nc.Block() as block:
            @block.sync
            def _(e):
                SP.replay(e)

            @block.gpsimd
            def _(e):
                POOL.replay(e)

            @block.scalar
            def _(e):
                ACT.replay(e)

            @block.vector
            def _(e):
                DVE.replay(e)

            @block.tensor
            def _(e):
                PE.replay(e)
    return nc


_PROGRAM_CACHE = {}


def make_in_maps(cfg, x_prompt, x_sample, p_prompt, p_sample, ln_mix_pre, w_in, q_norm, k_norm, na_rpb,
                 w_na_branch, w_gqa_branch, w_mix_out, ln_mix_post, ln_mlp_pre, w_ff1, w_ff2, ln_mlp_post,
                 ln_ple_pre, w_ple_gate, w_ple_proj, ln_ple_post):
    f32 = np.float32
    A = lambda a: np.ascontiguousarray(np.asarray(a, dtype=f32))
    x_prompt, x_sample, p_prompt, p_sample = A(x_prompt), A(x_sample), A(p_prompt)[0], A(p_sample)[0]
    D = cfg.D
    dri, dci, valid = _na_tables()
    rpb = A(na_rpb)[0]
    rpbG = np.where(valid[None], rpb[:, dri, dci], f32(0)).astype(f32)
    maskK = valid.astype(f32).astype(ml_dtypes.bfloat16)
    shared = {
        "w_in": A(w_in)[0], "w_na": A(w_na_branch)[0], "w_gqa": A(w_gqa_branch)[0], "w_mix": A(w_mix_out)[0],
        "w_ff1": A(w_ff1)[0], "w_ff2": A(w_ff2)[0], "w_gate": A(w_ple_gate)[0], "w_proj": A(w_ple_proj)[0],
        "ln_mix_pre": A(ln_mix_pre)[0], "ln_mix_post": A(ln_mix_post)[0], "ln_mlp_pre": A(ln_mlp_pre)[0],
        "ln_mlp_post": A(ln_mlp_post)[0], "ln_ple_pre": A(ln_ple_pre)[0], "ln_ple_post": A(ln_ple_post)[0],
        "q_norm": A(q_norm)[0], "k_norm": A(k_norm)[0], "rpbG": rpbG, "maskK": maskK, "A16": _A16(),
        "ident": np.eye(128, dtype=f32).astype(ml_dtypes.bfloat16), "permT": _perm_T(),
    }
    full_types = ["first"] + ["int"] * 6 + ["last"]
    rowB = [_rowB(full_types + ["first", "int", "int", "int"]), _rowB(full_types + ["int", "int", "int", "last"])]
    ropes = []
    for hidx in range(2):
        own = np.arange(hidx * HALF, (hidx + 1) * HALF)
        oth = np.arange((1 - hidx) * HALF, (2 - hidx) * HALF)
        pos = np.concatenate([np.arange(SEQ), own, oth])
        c, s = _rope_tables(pos)
        ropes.append(np.ascontiguousarray(np.stack([c, s], 0)))
    in_maps = []
    for c in range(8):
        sidx, hidx = c // 2, c % 2
        xs = x_sample[sidx]
        own = xs[hidx * HALF:(hidx + 1) * HALF]
        oth = xs[(1 - hidx) * HALF:(2 - hidx) * HALF]
        halo = np.zeros((512, D), f32)
        if hidx == 0:
            halo[256:512] = xs[HALF:HALF + 256]
        else:
            halo[0:256] = xs[HALF - 256:HALF]
        xall = np.concatenate([x_prompt[c], own, oth, halo], 0)
        pown = np.concatenate([p_prompt[c], p_sample[sidx][hidx * HALF:(hidx + 1) * HALF]], 0)
        m = dict(shared)
        m.update({"xall": np.ascontiguousarray(xall), "pown": np.ascontiguousarray(pown), "rowB": rowB[hidx],
                  "ropecs": ropes[hidx]})
        in_maps.append(m)
    return in_maps


def kernel(**inputs):
    return _run(Cfg(), inputs)


def _run(cfg, inputs):
    key = (cfg.D, cfg.NAH, cfg.GQH, cfg.KVH, cfg.DFF, cfg.PLE)
    if key not in _PROGRAM_CACHE:
        _PROGRAM_CACHE[key] = build_program(cfg)
    nc = _PROGRAM_CACHE[key]
    in_maps = make_in_maps(cfg, **inputs)
    res = run_bass_kernel_spmd(nc, in_maps, core_ids=list(range(8)))
    D = cfg.D
    y_prompt = np.empty((8, SEQ, D), np.float32)
    y_sample = np.empty((4, SEQ, D), np.float32)
    for c in range(8):
        o = np.asarray(res.results[c]["out"])
        y_prompt[c] = o[:SEQ]
        y_sample[c // 2, (c % 2) * HALF:(c % 2 + 1) * HALF] = o[SEQ:]
    return (y_prompt, y_sample)
```

```python
import contextlib
import numpy as np
import ml_dtypes
import concourse.bass as bass
import concourse.mybir as mybir
from concourse.bass_utils import run_bass_kernel_spmd

F32 = mybir.dt.float32
BF16 = mybir.dt.bfloat16
AF = mybir.ActivationFunctionType
ALU = mybir.AluOpType

GRID = 64
SEQ = 4096
HALF = 2048
HD = 128
EPS = 1e-6
ROPE_THETA = 10000.0
T = 512
NSLOT = 22
MASKV = -30000.0


class Cfg:
    def __init__(self, D=2048, NAH=8, GQH=8, KVH=2, DFF=8192, PLE=256):
        self.D, self.NAH, self.GQH, self.KVH, self.DFF, self.PLE = D, NAH, GQH, KVH, DFF, PLE
        self.G = GQH // KVH
        self.KC = D // 128
        self.NAW = NAH * HD
        self.GQW = GQH * HD
        self.KVW = KVH * HD
        self.IN_COLS = 3 * self.NAW + self.GQW + 2 * self.KVW + 2 * D
        assert self.IN_COLS % 512 == 0 and D % 512 == 0 and DFF % 1024 == 0
        assert self.G * 128 == 512
        kinds = []
        kinds += [("naq", h) for h in range(NAH)]
        kinds += [("nak", h) for h in range(NAH)]
        kinds += [("nav", h) for h in range(NAH)]
        kinds += [("gq", h) for h in range(GQH)]
        kinds += [("gk", h) for h in range(KVH)]
        kinds += [("gv", h) for h in range(KVH)]
        kinds += [("ga", m) for m in range(self.KC)]
        kinds += [("gb", m) for m in range(self.KC)]
        self.kinds = kinds
        self.TOK_OWN = SEQ + HALF
        self.NA_LOC = SEQ + HALF + 512
        self.GK_LOC = SEQ + SEQ
        self.N_XALL = SEQ + HALF + HALF + 512


class Prod:
    __slots__ = ("sem", "step", "count", "name")

    def __init__(self, sem, step, name=""):
        self.sem, self.step, self.count, self.name = sem, step, 0, name


class Res:
    __slots__ = ("name", "w", "r")

    def __init__(self, name=""):
        self.name, self.w, self.r = name, None, {}


class Eng:
    def __init__(self, name, sem, is_pe=False):
        self.name = name
        self.prod = Prod(sem, 1, name)
        self.seen = {}
        self.is_pe = is_pe
        self.ops = []
        self.dma_prods = []
        self.dma_rr = 0

    def _need(self, p, c):
        if p is self.prod and self.is_pe:
            return
        if self.seen.get(p, 0) >= c:
            return
        assert p.count >= c, f"forward wait {self.name} on {p.name}: {p.count} < {c}"
        self.ops.append(("w", p.sem, c * p.step))
        self.seen[p] = c

    def _sync(self, reads, writes):
        need = {}
        for r in reads:
            if r.w is not None:
                p, c = r.w
                if c > need.get(p, 0):
                    need[p] = c
        for w in writes:
            if w.w is not None:
                p, c = w.w
                if c > need.get(p, 0):
                    need[p] = c
            for p, c in w.r.items():
                if c > need.get(p, 0):
                    need[p] = c
        for p, c in need.items():
            self._need(p, c)

    def op(self, fn, reads=(), writes=(), inc=True):
        self._sync(reads, writes)
        if inc:
            self.prod.count += 1
            c = self.prod.count
            self.ops.append(("i", fn, self.prod.sem, 1))
        else:
            c = self.prod.count + 1
            self.ops.append(("i", fn, None, 0))
        for r in reads:
            r.r[self.prod] = c
        for w in writes:
            w.w = (self.prod, c)
            w.r = {}

    def dma(self, out_ap, in_ap, reads=(), writes=(), **kw):
        P = self.dma_prods[self.dma_rr % len(self.dma_prods)]
        self.dma_rr += 1
        if P.count > 0:
            self._need(P, P.count)
        self._sync(reads, writes)
        P.count += 1
        self.ops.append(("i", lambda e, o=out_ap, i=in_ap, kw=kw: e.dma_start(out=o, in_=i, **kw), P.sem, 16))
        for r in reads:
            r.r[P] = P.count
        for w in writes:
            w.w = (P, P.count)
            w.r = {}

    def replay(self, h):
        for o in self.ops:
            if o[0] == "w":
                h.wait_ge(o[1], o[2])
            else:
                inst = o[1](h)
                if o[2] is not None:
                    inst.then_inc(o[2], o[3])


def _rope_tables(pos):
    pos = np.asarray(pos)
    p = np.arange(128)
    w = p % 64
    f = w % 32
    freqs = (ROPE_THETA ** (-(2.0 * f.astype(np.float32)) / np.float32(64))).astype(np.float32)
    pr = (pos // GRID).astype(np.float32)
    pc = (pos % GRID).astype(np.float32)
    pp = np.where((p < 64)[:, None], pr[None, :], pc[None, :]).astype(np.float32)
    ang = (pp * freqs[:, None]).astype(np.float32)
    cos = np.cos(ang).astype(np.float32)
    sin = np.sin(ang).astype(np.float32)
    sgn = np.where(w < 32, -1.0, 1.0).astype(np.float32)[:, None]
    return cos, (sin * sgn).astype(np.float32)


def _perm_T():
    m = np.arange(128)
    src = np.where((m % 64) < 32, m + 32, m - 32)
    P = np.zeros((128, 128), np.float32)
    P[src, m] = 1.0
    return P


def _na_tables():
    a = np.arange(128) // 64
    kc = np.arange(128) % 64
    s = np.arange(NSLOT * 64) // 64
    qc = np.arange(NSLOT * 64) % 64
    dr = (10 - s)[None, :] + a[:, None]
    dc = kc[:, None] - qc[None, :]
    cs = np.clip(qc - 8, 0, GRID - 16)
    colv = (kc[:, None] >= cs[None, :]) & (kc[:, None] < cs[None, :] + 16)
    valid = (np.abs(dr) <= 7) & colv
    return np.clip(dr + 7, 0, 14), np.clip(dc + 15, 0, 30), valid


def _row_valid(bt):
    v = np.zeros((16, 8), bool)
    for i in range(8):
        st = {"int": i, "first": max(i, 4), "last": min(i, 4)}[bt]
        v[st:st + 8, i] = True
    return v


def _rowB(types):
    out = np.zeros((128, len(types), 512), np.float32)
    for b, bt in enumerate(types):
        v = _row_valid(bt)
        out[:16, b, :] = np.repeat(np.where(v, 0.0, MASKV), 64, axis=1)
    return out.astype(ml_dtypes.bfloat16)


def _A16():
    A = np.zeros((128, 8, 128), np.float32)
    for j in range(8):
        for key in range(128):
            A[2 * j + key // 64, j, key] = 1.0
    return A.astype(ml_dtypes.bfloat16)


def build_program(cfg):
    D, KC, NAH, GQH, KVH, G, DFF, PLE = cfg.D, cfg.KC, cfg.NAH, cfg.GQH, cfg.KVH, cfg.G, cfg.DFF, cfg.PLE
    NT_IN = cfg.IN_COLS // 512
    NT_D = D // 512
    PKC = PLE // 128
    NAKC = cfg.NAW // 128
    GQKC = cfg.GQW // 128
    FF_HALF = DFF // 2
    UCH = FF_HALF // 128
    KGS = min(16, UCH)
    KG = UCH // KGS
    assert UCH % KGS == 0
    HG = min(4, NAH)
    NHG = NAH // HG
    SCALE = float(HD) ** -0.5

    nc = bass.Bass("TRN2", target_bir_lowering=False)

    def din(name, shape, dt=F32):
        return nc.dram_tensor(name, list(shape), dt, kind="ExternalInput").ap()

    def dscr(name, shape, dt=BF16):
        return nc.dram_tensor(name, list(shape), dt, kind="Internal").ap()

    xall = din("xall", [cfg.N_XALL, D])
    pown = din("pown", [cfg.TOK_OWN, PLE])
    w_in = din("w_in", [D, cfg.IN_COLS])
    w_na = din("w_na", [cfg.NAW, D])
    w_gqa = din("w_gqa", [cfg.GQW, D])
    w_mix = din("w_mix", [D, D])
    w_ff1 = din("w_ff1", [D, DFF])
    w_ff2 = din("w_ff2", [DFF, D])
    w_gate = din("w_gate", [D, D])
    w_proj = din("w_proj", [PLE, D])
    gains = {n: din(n, [D]) for n in ["ln_mix_pre", "ln_mix_post", "ln_mlp_pre", "ln_mlp_post", "ln_ple_pre",
                                      "ln_ple_post"]}
    q_norm = din("q_norm", [128])
    k_norm = din("k_norm", [128])
    rpbG = din("rpbG", [NAH, 128, NSLOT * 64])
    maskK = din("maskK", [128, NSLOT * 64], BF16)
    ropecs = din("ropecs", [2, 128, cfg.GK_LOC])
    rowB_d = din("rowB", [128, 12, 512], BF16)
    A16_d = din("A16", [128, 8, 128], BF16)
    ident_d = din("ident", [128, 128], BF16)
    permT_d = din("permT", [128, 128])
    out = nc.dram_tensor("out", [cfg.TOK_OWN, D], F32, kind="ExternalOutput").ap()

    wb = {
        "in": dscr("wb_in", [NT_IN, 128, KC, 512]),
        "na": dscr("wb_na", [NT_D, 128, NAKC, 512]),
        "gqa": dscr("wb_gqa", [NT_D, 128, GQKC, 512]),
        "mix": dscr("wb_mix", [NT_D, 128, KC, 512]),
        "ff1": dscr("wb_ff1", [DFF // 512, 128, KC, 512]),
        "ff2": dscr("wb_ff2", [NT_D, 128, DFF // 128, 512]),
        "gate": dscr("wb_gate", [NT_D, 128, KC, 512]),
        "proj": dscr("wb_proj", [NT_D, 128, PKC, 512]),
    }
    wsrc = {"in": w_in, "na": w_na, "gqa": w_gqa, "mix": w_mix, "ff1": w_ff1, "ff2": w_ff2, "gate": w_gate,
            "proj": w_proj}
    naqT = dscr("naqT", [NAH, 128, cfg.TOK_OWN])
    nakT = dscr("nakT", [NAH, 128, cfg.NA_LOC])
    nav = dscr("nav", [cfg.NA_LOC, cfg.NAW])
    gqT = dscr("gqT", [GQH, 128, cfg.TOK_OWN])
    gkT = dscr("gkT", [KVH, 128, cfg.GK_LOC])
    gv = dscr("gv", [cfg.GK_LOC, cfg.KVW])
    sgaT = dscr("sgaT", [KC, 128, cfg.TOK_OWN])
    sgbT = dscr("sgbT", [KC, 128, cfg.TOK_OWN])
    aT_d = naqT
    bT_d = gqT

    es = contextlib.ExitStack()
    with es:
        def sb(name, shape, dt):
            return es.enter_context(nc.sbuf_tensor("sb_" + name, list(shape), dt))

        def sem(name):
            return es.enter_context(nc.semaphore(name))

        PE = Eng("pe", sem("s_pe"), is_pe=True)
        ACT = Eng("act", sem("s_act"))
        DVE = Eng("dve", sem("s_dve"))
        POOL = Eng("pool", sem("s_pool"))
        SP = Eng("sp", sem("s_sp"))
        SP.dma_prods = [Prod(sem(f"s_spd{i}"), 16, f"spd{i}") for i in range(40)]
        POOL.dma_prods = [Prod(sem(f"s_pld{i}"), 16, f"pld{i}") for i in range(40)]
        engines = [PE, ACT, DVE, POOL, SP]

        def barrier():
            prods = [e.prod for e in engines] + SP.dma_prods + POOL.dma_prods
            for e in engines:
                for p in prods:
                    if p.count > 0 and not (p is e.prod) and e.seen.get(p, 0) < p.count:
                        e.ops.append(("w", p.sem, p.count * p.step))
                        e.seen[p] = p.count

        ident = sb("ident", [128, 128], BF16)
        permT = sb("permT", [128, 128], F32)
        onesf = sb("onesf", [128, 128], F32)
        onesb = sb("onesb", [128, 128], BF16)
        ones1 = sb("ones1", [128, 128], F32)
        gsb = {n: sb("g_" + n, [128, KC], F32) for n in ["ln_mix_pre", "ln_mlp_pre", "ln_ple_pre"]}
        qn = sb("qn", [128, 1], F32)
        kn = sb("kn", [128, 1], F32)
        NWS = 3
        wring = [sb(f"wring{i}", [128, 16, 512], BF16) for i in range(NWS)]
        r_wring = [Res(f"wring{i}") for i in range(NWS)]
        r_wring2 = [Res(f"wring2_{i}") for i in range(NWS)]
        stat = sb("stat", [128, 16], F32)
        r_stat = Res("stat")
        r_stat_a = [Res(f"stat_a{i}") for i in range(4)]
        r_stat_b = [Res(f"stat_b{i}") for i in range(4)]
        r_const = Res("const")
        ps = [es.enter_context(nc.psum_tensor(f"ps{i}", [128, 512], F32)) for i in range(8)]
        r_ps = [Res(f"ps{i}") for i in range(8)]
        bank_rr = [0]
        bank_mod = [8, 0]

        def bank():
            i = bank_mod[1] + bank_rr[0] % bank_mod[0]
            bank_rr[0] += 1
            return ps[i], r_ps[i]

        ARENA_BYTES = 154 * 1024
        arena = sb("arena", [128, ARENA_BYTES // 2], BF16)

        class Carver:
            def __init__(self):
                self.off = 0

            def take(self, shape, dt):
                n = int(np.prod(shape))
                esz = 2 if dt == BF16 else 4
                self.off = (self.off + 63) // 64 * 64
                o = self.off
                self.off += n * esz
                assert self.off <= ARENA_BYTES, f"arena overflow {self.off}"
                a = arena[:, o // 2:(o + n * esz) // 2]
                if dt == F32:
                    a = a.bitcast(F32)
                if len(shape) == 2:
                    a = a.rearrange("p (a b) -> p a b", b=shape[1])
                elif len(shape) == 3:
                    a = a.rearrange("p (a b c) -> p a b c", b=shape[1], c=shape[2])
                return a

        SP.dma(ident[:], ident_d, writes=[r_const])
        SP.dma(permT[:], permT_d, writes=[r_const])
        SP.dma(qn[:], q_norm.rearrange("(p o) -> p o", o=1), writes=[r_const])
        SP.dma(kn[:], k_norm.rearrange("(p o) -> p o", o=1), writes=[r_const])
        for n in gsb:
            SP.dma(gsb[n][:], gains[n].rearrange("(kc p) -> p kc", p=128), writes=[r_const],
                   allow_slow_non_contiguous=True)
        DVE.op(lambda e: e.memset(onesf[:], 1.0 / 128), writes=[r_const])
        DVE.op(lambda e: e.memset(onesb[:], 1.0), writes=[r_const])
        DVE.op(lambda e: e.memset(ones1[:], 1.0), writes=[r_const])

        r_wb = {}
        for name in wb:
            nt_n, _, kct, _ = wb[name].shape
            r_wb[name] = [[Res(f"wb_{name}{i}_{g}") for g in range((kct + 15) // 16)] for i in range(nt_n)]

        def cast_list(name):
            w, dst = wsrc[name], wb[name]
            nt_n, _, kct, _ = dst.shape
            src = w.rearrange("(kc p) (nt c) -> nt p kc c", p=128, c=512)
            lst = []
            for nt in range(nt_n):
                for gi, k0 in enumerate(range(0, kct, 16)):
                    k1 = min(kct, k0 + 16)
                    lst.append((dst[nt, :, k0:k1, :], src[nt, :, k0:k1, :], r_wb[name][nt][gi]))
            return lst

        def cast_emit(lst):
            for d_ap, s_ap, r in lst:
                POOL.dma(d_ap, s_ap, writes=[r])

        def cast_weight(name):
            cast_emit(cast_list(name))

        rest_casts = []
        for _n in ["na", "gqa", "mix", "ff1", "ff2", "gate", "proj"]:
            rest_casts.extend(cast_list(_n))
        N_OWN_TILES = cfg.TOK_OWN // T
        CASTS_PER_TILE = (len(rest_casts) + N_OWN_TILES - 2) // (N_OWN_TILES - 1)

        barrier()
        r_const.w = None

        def tiles_for(kindset):
            return [nt for nt in range(NT_IN) if any(cfg.kinds[nt * 4 + c][0] in kindset for c in range(4))]

        ALLK = {"naq", "nak", "nav", "gq", "gk", "gv", "ga", "gb"}
        p1_tiles = []
        for i in range(SEQ // T):
            p1_tiles.append(dict(x0=i * T, kinds=ALLK, own=i * T, na=i * T, gk=i * T))
        for i in range(HALF // T):
            p1_tiles.append(dict(x0=SEQ + i * T, kinds=ALLK, own=SEQ + i * T, na=SEQ + i * T, gk=SEQ + i * T))
        for i in range(HALF // T):
            p1_tiles.append(dict(x0=SEQ + HALF + i * T, kinds={"gk", "gv"}, own=None, na=None,
                                 gk=SEQ + HALF + i * T))
        p1_tiles.append(dict(x0=SEQ + 2 * HALF, kinds={"nak", "nav"}, own=None, na=SEQ + HALF, gk=None))
        for tl in p1_tiles:
            tl["wtiles"] = tiles_for(tl["kinds"])
        p1_tiles = [tl for tl in p1_tiles if tl["own"] is None] + [tl for tl in p1_tiles if tl["own"] is not None]
        _n_part = sum(1 for tl in p1_tiles if tl["own"] is None)
        for _i, tl in enumerate(p1_tiles):
            tl["pool_prefetch"] = _i >= _n_part + 2
        first_need = []
        for tl in p1_tiles:
            for nt in tl["wtiles"]:
                if nt not in first_need:
                    first_need.append(nt)

        _cl = cast_list("in")
        assert len(_cl) == NT_IN
        cast_emit([_cl[nt] for nt in first_need])

        wseq = []
        for tl in p1_tiles:
            for nt in tl["wtiles"]:
                wseq.append((wb["in"][nt], KC, r_wb["in"][nt]))
        for t in range(cfg.TOK_OWN // T):
            for nt in range(NT_D):
                wseq.append((wb["na"][nt], NAKC, r_wb["na"][nt], wb["gqa"][nt], GQKC, r_wb["gqa"][nt]))
            for nt in range(NT_D):
                wseq.append((wb["mix"][nt], KC, r_wb["mix"][nt]))
            for half in range(2):
                for nt in range(FF_HALF // 512):
                    i = half * (FF_HALF // 512) + nt
                    wseq.append((wb["ff1"][i], KC, r_wb["ff1"][i]))
                for n in range(NT_D):
                    for kg in range(KG):
                        kk = (half * KG + kg) * KGS
                        wseq.append((wb["ff2"][n, :, kk:kk + KGS, :], KGS, [r_wb["ff2"][n][kk // 16]]))
            for nt in range(NT_D):
                wseq.append((wb["gate"][nt], KC, r_wb["gate"][nt]))
            wseq.append((wb["proj"].rearrange("n p k c -> p n k c"), NT_D * PKC,
                         [r for nt in range(NT_D) for r in r_wb["proj"][nt]]))
        wstate = {"emitted": 0, "taken": 0}

        def wget(held_prev=0):
            i = wstate["taken"]
            lim = min(len(wseq), i + NWS - held_prev)
            while wstate["emitted"] < lim:
                j = wstate["emitted"]
                ent = wseq[j]
                src, kc, rs = ent[0], ent[1], ent[2]
                sl = j % NWS
                SP._sync([], [r_wring[sl], r_wring2[sl]])
                dst = wring[sl][:, 0:kc, :]
                if len(src.shape) == 4:
                    dst = dst.rearrange("p (n k) c -> p n k c", k=src.shape[2])
                SP.dma(dst, src, reads=rs, writes=[r_wring[sl]])
                if len(ent) == 6:
                    SP.dma(wring[sl][:, kc:kc + ent[4], :], ent[3], reads=ent[5], writes=[r_wring2[sl]])
                wstate["emitted"] += 1
            wstate["taken"] += 1
            if len(wseq[i]) == 6:
                return wring[i % NWS], r_wring[i % NWS], r_wring2[i % NWS]
            return wring[i % NWS], r_wring[i % NWS]

        cp_rr = [0]

        def evac_copy(out_ap, in_ap, reads, writes):
            cp_rr[0] += 1
            if cp_rr[0] % 2:
                DVE.op(lambda e: e.tensor_copy(out=out_ap, in_=in_ap), reads=reads, writes=writes)
            else:
                ACT.op(lambda e: e.activation(out=out_ap, in_=in_ap, func=AF.Copy), reads=reads, writes=writes)

        def norm_scale(src, r_src, xn_, r_xn_, sq_split=True):
            grp = [list(r) if isinstance(r, (list, tuple)) else [r] for r in r_xn_]
            for s in range(4):
                if s < 2 or not sq_split:
                    ACT.op(lambda e, s=s: e.activation(out=xn_[:, s, :], in_=src[:, s, :], func=AF.Square,
                                                       scale=float(D) ** -0.5, accum_out=stat[:, s:s + 1]),
                           reads=[r_src[s]], writes=grp[s] + [r_stat_a[s]])
                else:
                    DVE.op(lambda e, s=s: e.scalar_tensor_tensor(out=xn_[:, s, :], in0=src[:, s, :], scalar=1.0 / D,
                                                                 in1=src[:, s, :], op0=ALU.mult, op1=ALU.mult,
                                                                 accum_out=stat[:, s:s + 1]),
                           reads=[r_src[s]], writes=grp[s] + [r_stat_a[s]])
            ACT.op(lambda e: e.activation(out=stat[:, 0:4], in_=stat[:, 0:4], func=AF.Sqrt, bias=EPS),
                   reads=r_stat_a, writes=[r_stat])
            DVE.op(lambda e: e.reciprocal(out=stat[:, 4:8], in_=stat[:, 0:4]), reads=[r_stat], writes=[r_stat])
            for s in (0, 1):
                ACT.op(lambda e, s=s: e.activation(out=xn_[:, s, :], in_=src[:, s, :], func=AF.Copy,
                                                   scale=stat[:, 4 + s:5 + s]),
                       reads=[r_src[s], r_stat], writes=grp[s])
            for s in (2, 3):
                DVE.op(lambda e, s=s: e.tensor_scalar(out=xn_[:, s, :], in0=src[:, s, :],
                                                      scalar1=stat[:, 4 + s:5 + s], scalar2=None, op0=ALU.mult),
                       reads=[r_src[s], r_stat], writes=grp[s])

        def transpose_evac(xn_, r_xn_, hT_, r_hT_, gain):
            grp = [list(r) if isinstance(r, (list, tuple)) else [r] for r in r_xn_]
            for k in range(KC):
                pb, rb = bank()
                pbb = pb[:].bitcast(BF16)
                for s in range(4):
                    PE.op(lambda e, s=s, k=k, pbb=pbb: e.transpose(out=pbb[:, s * 128:(s + 1) * 128],
                                                                  in_=xn_[:, s, k * 128:(k + 1) * 128],
                                                                  identity=ident[:]),
                          reads=grp[s], writes=[rb], inc=(s == 3))
                if k % 2 == 0:
                    DVE.op(lambda e, k=k, pbb=pbb: e.tensor_scalar(out=hT_[:, k, :], in0=pbb[:, 0:512],
                                                                  scalar1=gain[:, k:k + 1], scalar2=None,
                                                                  op0=ALU.mult),
                           reads=[rb], writes=[r_hT_[k]])
                else:
                    ACT.op(lambda e, k=k, pbb=pbb: e.activation(out=hT_[:, k, :], in_=pbb[:, 0:512], func=AF.Copy,
                                                               scale=gain[:, k:k + 1]),
                           reads=[rb], writes=[r_hT_[k]])

        def norm_transpose(src, r_src, xn_, r_xn_, hT_, r_hT_, gain, sq_split=True):
            norm_scale(src, r_src, xn_, r_xn_, sq_split)
            transpose_evac(xn_, r_xn_, hT_, r_hT_, gain)

        cv = Carver()
        xt = cv.take([4, D], F32)
        r_xt = [Res(f"xt{i}") for i in range(4)]
        xn = cv.take([4, D], BF16)
        r_xn = [Res(f"xn{s}") for s in range(4)]
        hT_b = [cv.take([KC, 512], BF16) for _ in range(2)]
        r_hT_b = [[Res(f"hT{i}_{k}") for k in range(KC)] for i in range(2)]
        NSTG = 8
        stg = [cv.take([4, 512], BF16) for _ in range(NSTG)]
        r_stg = [[Res(f"stg{i}_{c}") for c in range(4)] for i in range(NSTG)]
        cst = cv.take([2, 512], F32)
        r_cst = Res("cst")
        t_sq = [cv.take([512], F32) for _ in range(2)]
        t_gq = [cv.take([512], F32) for _ in range(2)]
        t_rs = [cv.take([512], F32) for _ in range(2)]
        t_t1 = [cv.take([512], F32) for _ in range(2)]
        r_tq = [[Res(f"tq{i}_{j}") for j in range(4)] for i in range(2)]
        stg_rr = [0]
        rope_rr = [0]

        scr = {"naq": naqT, "nak": nakT, "gq": gqT, "gk": gkT, "ga": sgaT, "gb": sgbT}
        FEAT = {"naq", "nak", "gq", "gk", "ga", "gb"}

        def dst_off(tl, kind):
            if kind in ("naq", "gq", "ga", "gb"):
                return tl["own"]
            if kind in ("nak", "nav"):
                return tl["na"]
            return tl["gk"]

        rope_pend = []

        def rope_chunk(pb, rb, gain_ap, out_ap, r_out):
            i = rope_rr[0] % 2
            rope_rr[0] += 1
            sq, gq_ = t_sq[i], t_gq[i]
            rq = r_tq[i]
            ACT.op(lambda e: e.activation(out=sq, in_=pb[:], func=AF.Square), reads=[rb], writes=[rq[0]])
            ACT.op(lambda e: e.activation(out=gq_, in_=pb[:], func=AF.Copy, scale=gain_ap),
                   reads=[rb], writes=[rq[1]])
            rope_pend.append((i, out_ap, r_out))

        def rope_flush(keep=0):
            while len(rope_pend) > keep:
                i, out_ap, r_out = rope_pend.pop(0)
                sq, gq_, rs, t1 = t_sq[i], t_gq[i], t_rs[i], t_t1[i]
                rq = r_tq[i]
                p_ss, r_ss = bank()
                PE.op(lambda e, p_ss=p_ss, sq=sq: e.matmul(p_ss[:], lhsT=onesf[:], rhs=sq, start=True, stop=True),
                      reads=[rq[0]], writes=[r_ss])
                p_rt, r_rt = bank()
                PE.op(lambda e, p_rt=p_rt, gq_=gq_: e.matmul(p_rt[:], lhsT=permT[:], rhs=gq_, start=True, stop=True),
                      reads=[rq[1]], writes=[r_rt])
                ACT.op(lambda e, rs=rs, p_ss=p_ss: e.activation(out=rs, in_=p_ss[:], func=AF.Sqrt, bias=EPS),
                       reads=[r_ss], writes=[rq[2]])
                DVE.op(lambda e, rs=rs: e.reciprocal(out=rs, in_=rs), reads=[rq[2]], writes=[rq[2]])
                DVE.op(lambda e, t1=t1, p_rt=p_rt: e.tensor_tensor(out=t1, in0=p_rt[:], in1=cst[:, 1, :], op=ALU.mult),
                       reads=[r_rt, r_cst], writes=[rq[3]])
                DVE.op(lambda e, gq_=gq_: e.tensor_tensor(out=gq_, in0=gq_, in1=cst[:, 0, :], op=ALU.mult),
                       reads=[rq[1], r_cst], writes=[rq[1]])
                DVE.op(lambda e, t1=t1, gq_=gq_: e.tensor_tensor(out=t1, in0=t1, in1=gq_, op=ALU.add),
                       reads=[rq[3], rq[1]], writes=[rq[3]])
                DVE.op(lambda e, t1=t1, rs=rs, out_ap=out_ap: e.tensor_tensor(out=out_ap, in0=t1, in1=rs, op=ALU.mult),
                       reads=[rq[3], rq[2]], writes=[r_out])

        def p1_xload(tl, q=None):
            x0 = tl["x0"]
            (q or SP).dma(xt, xall[x0:x0 + T, :].rearrange("(s p) d -> p s d", p=128), writes=r_xt)

        def phase1_tile(tl, next_tl, next2_tl, hT, r_hT):
            if tl["kinds"] & {"gq", "gk"}:
                g0 = tl["gk"]
                SP.dma(cst, ropecs[:, :, g0:g0 + T].rearrange("c p t -> p c t"), writes=[r_cst])
            transpose_evac(xn, r_xn, hT, r_hT, gsb["ln_mix_pre"])
            mid = (len(tl["wtiles"]) * 2) // 3
            for wi_, nt in enumerate(tl["wtiles"]):
                if wi_ == mid and next_tl is not None:
                    norm_scale(xt, r_xt, xn, r_xn)
                    if next2_tl is not None:
                        p1_xload(next2_tl, POOL if tl.get("pool_prefetch") else SP)
                wt, r_wt = wget()
                if tl["own"] is not None and rest_casts and wi_ % 3 == 2:
                    cast_emit(rest_casts[:1])
                    del rest_casts[:1]
                runs = []
                for c in range(4):
                    kind, idx = cfg.kinds[nt * 4 + c]
                    if kind not in tl["kinds"]:
                        continue
                    if runs and runs[-1][0] == kind and runs[-1][2] + runs[-1][3] == c:
                        runs[-1][3] += 1
                    else:
                        runs.append([kind, idx, c, 1])
                for (kind, idx0, c0, n) in runs:
                    off = dst_off(tl, kind)
                    si = stg_rr[0] % NSTG
                    stg_rr[0] += 1
                    sg, r_sg = stg[si], r_stg[si]
                    if kind in FEAT:
                        for cc in range(n):
                            c = c0 + cc
                            pb, rb = bank()
                            for k in range(KC):
                                PE.op(lambda e, k=k, c=c, pb=pb, wt=wt: e.matmul(
                                    pb[:], lhsT=wt[:, k, c * 128:(c + 1) * 128], rhs=hT[:, k, :],
                                    start=(k == 0), stop=(k == KC - 1)),
                                    reads=[r_wt, r_hT[k]], writes=[rb], inc=(k == KC - 1))
                            o_ap = sg[:, c, :]
                            if kind in ("naq", "nak"):
                                evac_copy(o_ap, pb[:], [rb], [r_sg[c]])
                            elif kind in ("ga", "gb"):
                                ACT.op(lambda e, o_ap=o_ap, pb=pb: e.activation(out=o_ap, in_=pb[:],
                                                                                 func=AF.Sigmoid),
                                       reads=[rb], writes=[r_sg[c]])
                            else:
                                rope_flush(keep=0)
                                rope_chunk(pb, rb, (qn if kind == "gq" else kn)[:, 0:1], o_ap, r_sg[c])
                        rope_flush()
                        POOL.dma(scr[kind][idx0:idx0 + n, :, off:off + T].rearrange("h p t -> p h t"),
                                 sg[:, c0:c0 + n, :], reads=r_sg[c0:c0 + n])
                    else:
                        dst = nav if kind == "nav" else gv
                        for s in range(4):
                            pb, rb = bank()
                            for k in range(KC):
                                PE.op(lambda e, k=k, s=s, pb=pb, wt=wt, c0=c0, n=n: e.matmul(
                                    pb[:, 0:n * 128], lhsT=hT[:, k, s * 128:(s + 1) * 128],
                                    rhs=wt[:, k, c0 * 128:(c0 + n) * 128], start=(k == 0), stop=(k == KC - 1)),
                                    reads=[r_wt, r_hT[k]], writes=[rb], inc=(k == KC - 1))
                            evac_copy(sg[:, s, c0 * 128:(c0 + n) * 128], pb[:, 0:n * 128], [rb], [r_sg[s]])
                        POOL.dma(dst[off:off + T, idx0 * 128:(idx0 + n) * 128].rearrange("(s p) c -> p s c", p=128),
                                 sg[:, :, c0 * 128:(c0 + n) * 128], reads=r_sg)

        p1_xload(p1_tiles[0])
        norm_scale(xt, r_xt, xn, r_xn)
        p1_xload(p1_tiles[1])
        for ti, tl in enumerate(p1_tiles):
            phase1_tile(tl, p1_tiles[ti + 1] if ti + 1 < len(p1_tiles) else None,
                        p1_tiles[ti + 2] if ti + 2 < len(p1_tiles) else None, hT_b[ti % 2], r_hT_b[ti % 2])
        cast_emit(rest_casts)
        del rest_casts[:]

        barrier()

        cv = Carver()
        m3r = cv.take([NAH, NSLOT * 64], BF16)
        r_m3r = Res("m3r")
        mk = cv.take([NSLOT * 64], BF16)
        r_mk = Res("mk")
        rtmp = cv.take([NSLOT * 64], F32)
        r_rtmp = Res("rtmp")
        A16 = cv.take([8, 128], BF16)
        rowB = cv.take([12, 512], BF16)
        r_c2 = Res("c2")
        r_c2a = Res("c2a")
        NWIN = 2
        kwin = [cv.take([HG, 1024], BF16) for _ in range(NWIN)]
        vwin = [cv.take([8, HG * 128], BF16) for _ in range(NWIN)]
        qwin = [cv.take([HG, 512], BF16) for _ in range(NWIN)]
        r_kw = [[Res(f"kw{i}_{j}") for j in range(3)] for i in range(NWIN)]
        r_vw = [[Res(f"vw{i}_{j}") for j in range(3)] for i in range(NWIN)]
        r_qw = [Res(f"qw{i}") for i in range(NWIN)]
        ostg = [cv.take([max(NAH, GQH), 512], BF16) for _ in range(2)]
        r_ostg = [Res(f"ostg{i}") for i in range(2)]
        NPT = 4
        est = [cv.take([512], BF16) for _ in range(NPT)]
        r_est = [Res(f"est{i}") for i in range(NPT)]
        ptt = [cv.take([512], BF16) for _ in range(NPT)]
        r_ptt = [Res(f"ptt{i}") for i in range(NPT)]
        rec = [cv.take([512], F32) for _ in range(2)]
        r_rec = [Res(f"rec{i}") for i in range(2)]

        SP.dma(A16, A16_d, writes=[r_c2a])
        SP.dma(rowB, rowB_d, writes=[r_c2])
        na_blocks = []
        for b in range(8):
            R0 = 8 * b
            v0, v1 = R0 - 4, R0 + 12
            lo, hi = max(v0, 0), min(v1, GRID)
            pieces = [(lo * GRID, (hi - lo) * GRID, (lo - v0) * GRID)]
            chunks = [j for j in range(8) if (v0 + 2 * j) >= 0 and (v0 + 2 * j + 1) < GRID]
            na_blocks.append(dict(q0=R0 * GRID, pieces=pieces, chunks=chunks, bidx=b))
        for b in range(4):
            v0 = 8 * b
            pieces = []
            for vr0, vr1 in [(0, 4), (4, 36), (36, 40)]:
                lo, hi = max(v0, vr0), min(v0 + 16, vr1)
                if lo < hi:
                    if vr0 == 0:
                        src = SEQ + HALF + lo * GRID
                    elif vr0 == 4:
                        src = SEQ + (lo - 4) * GRID
                    else:
                        src = SEQ + HALF + 256 + (lo - 36) * GRID
                    pieces.append((src, (hi - lo) * GRID, (lo - v0) * GRID))
            na_blocks.append(dict(q0=SEQ + 8 * b * GRID, pieces=pieces, chunks=list(range(8)), bidx=8 + b))
        na_units = [(blk, hg) for blk in na_blocks for hg in range(NHG)]

        def na_load(unit, wi):
            blk, hg = unit
            h0 = hg * HG
            for pi, (src, n, dof) in enumerate(blk["pieces"]):
                SP.dma(kwin[wi][:, :, dof:dof + n], nakT[h0:h0 + HG, :, src:src + n].rearrange("h p t -> p h t"),
                       writes=[r_kw[wi][pi]])
                SP.dma(vwin[wi][:, dof // 128:(dof + n) // 128, :],
                       nav[src:src + n, h0 * 128:(h0 + HG) * 128].rearrange("(j p) c -> p j c", p=128),
                       writes=[r_vw[wi][pi]])
            q0 = blk["q0"]
            SP.dma(qwin[wi], naqT[h0:h0 + HG, :, q0:q0 + 512].rearrange("h p t -> p h t"), writes=[r_qw[wi]])

        pt_rr = [0]
        rec_rr = [0]
        ostg_rr = [0]
        acc_rr = [0]
        bank_mod[0], bank_mod[1] = 4, 4

        LOOK = 3
        apend = []

        accs = {}

        def attn_group(nchunks, r_v, out_ap, r_out, view=None, done_cb=None, den="pe"):
            par = acc_rr[0] % 2
            a = par * 2
            acc_rr[0] += 1
            g = dict(n=nchunks, p_o=ps[a], r_o=r_ps[a], p_d=ps[a + 1], r_d=r_ps[a + 1], r_v=r_v, out_ap=out_ap,
                     r_out=r_out, view=view, done_cb=done_cb, den=den)
            if den == "vec":
                g["acc"] = accs["ap"][par]
                g["r_acc"] = accs["res"][par]
                g["dc"] = 0
                g["init"] = set()
            return g

        def attn_pv(step):
            g, idx, i, lhsT_ap, pt_ap, r_pt = step
            p_o, r_o, p_d, r_d, n = g["p_o"], g["r_o"], g["p_d"], g["r_d"], g["n"]
            if g["den"] == "pe":
                PE.op(lambda e: e.matmul(p_o[:], lhsT=lhsT_ap, rhs=pt_ap, start=(idx == 0), stop=(idx == n - 1)),
                      reads=g["r_v"] + [r_pt], writes=[r_o], inc=False)
                PE.op(lambda e: e.matmul(p_d[:], lhsT=onesb[:], rhs=pt_ap, start=(idx == 0), stop=(idx == n - 1)),
                      reads=[r_pt], writes=[r_d, r_o], inc=True)
            else:
                on_pe = (idx % 2 == 1)
                PE.op(lambda e: e.matmul(p_o[:], lhsT=lhsT_ap, rhs=pt_ap, start=(idx == 0), stop=(idx == n - 1)),
                      reads=g["r_v"] + [r_pt], writes=[r_o], inc=not on_pe)
                if on_pe:
                    PE.op(lambda e: e.matmul(p_d[:], lhsT=onesb[:], rhs=pt_ap, start=(idx == 1), stop=False),
                          reads=[r_pt], writes=[r_d, r_o], inc=True)
                else:
                    if idx % 8 == 6:
                        ai, eng = 2, POOL
                    else:
                        ai, eng = g["dc"] % 2, DVE
                        g["dc"] += 1
                    acc_ap, r_acc = g["acc"][ai], g["r_acc"][ai]
                    first_use = ai not in g["init"]
                    g["init"].add(ai)
                    if first_use:
                        eng.op(lambda e: e.tensor_copy(out=acc_ap, in_=pt_ap), reads=[r_pt], writes=[r_acc])
                    else:
                        eng.op(lambda e: e.tensor_tensor(out=acc_ap, in0=acc_ap, in1=pt_ap, op=ALU.add),
                               reads=[r_pt, r_acc], writes=[r_acc])
                if idx == n - 1:
                    assert g["init"] == {0, 1, 2} and n > 1
                    for ai2 in range(3):
                        a_ap = g["acc"][ai2]
                        PE.op(lambda e, a_ap=a_ap, ai2=ai2: e.matmul(p_d[:], lhsT=ones1[:], rhs=a_ap,
                                                                     start=False, stop=(ai2 == 2)),
                              reads=[g["r_acc"][ai2]], writes=[r_d], inc=(ai2 == 2))
            if idx == n - 1:
                ri = rec_rr[0] % 2
                rec_rr[0] += 1
                rec_ap, r_rc = rec[ri], r_rec[ri]
                out_ap, view = g["out_ap"], g["view"]
                ACT.op(lambda e: e.activation(out=rec_ap, in_=p_d[:], func=AF.Ln), reads=[r_d], writes=[r_rc])
                ACT.op(lambda e: e.activation(out=rec_ap, in_=rec_ap, func=AF.Exp, scale=-1.0), reads=[r_rc],
                       writes=[r_rc])
                if view is None:
                    DVE.op(lambda e: e.tensor_tensor(out=out_ap, in0=p_o[:], in1=rec_ap, op=ALU.mult),
                           reads=[r_o, r_rc], writes=[g["r_out"]])
                else:
                    DVE.op(lambda e: e.tensor_tensor(out=out_ap, in0=view(p_o[:]), in1=view(rec_ap), op=ALU.mult),
                           reads=[r_o, r_rc], writes=[g["r_out"]])
                if g["done_cb"] is not None:
                    g["done_cb"]()

        def attn_step(g, idx, qk_fn, post_fn, lhsT_ap):
            p_s, r_s = bank()
            qk_fn(idx, p_s, r_s)
            i = pt_rr[0] % NPT
            pt_rr[0] += 1
            post_fn(idx, p_s, r_s, i)
            apend.append((g, idx, i, lhsT_ap, ptt[i], r_ptt[i]))
            while len(apend) > LOOK:
                attn_pv(apend.pop(0))

        def attn_flush():
            while apend:
                attn_pv(apend.pop(0))

        na_load(na_units[0], 0)
        SP.dma(mk, maskK, writes=[r_mk])
        for h in range(NAH):
            SP.dma(rtmp, rpbG[h], writes=[r_rtmp])
            ACT.op(lambda e: e.activation(out=rtmp, in_=rtmp, func=AF.Exp), reads=[r_rtmp], writes=[r_rtmp])
            DVE.op(lambda e, h=h: e.tensor_tensor(out=m3r[:, h, :], in0=rtmp, in1=mk, op=ALU.mult),
                   reads=[r_rtmp, r_mk], writes=[r_m3r])

        for ui, (blk, hg) in enumerate(na_units):
            wi = ui % NWIN
            if hg == 0:
                oi = ostg_rr[0] % 2
                ostg_rr[0] += 1
            chunks = blk["chunks"]
            r_kq = r_kw[wi] + [r_qw[wi]]
            for hl in range(HG):
                h = hg * HG + hl

                def qk_fn(jx, p_s, r_s, hl=hl, wi=wi, blk=blk, chunks=chunks, r_kq=r_kq):
                    j = chunks[jx]
                    PE.op(lambda e: e.matmul(p_s[:], lhsT=kwin[wi][:, hl, j * 128:(j + 1) * 128],
                                             rhs=qwin[wi][:, hl, :], start=True, stop=False),
                          reads=r_kq, writes=[r_s], inc=False)
                    PE.op(lambda e: e.matmul(p_s[:], lhsT=A16[:, j, :], rhs=rowB[:, blk["bidx"], :],
                                             start=False, stop=True),
                          reads=[r_c2, r_c2a], writes=[r_s], inc=True)

                def post_fn(jx, p_s, r_s, i, h=h, chunks=chunks):
                    j = chunks[jx]
                    es_ap, pt_ap = est[i], ptt[i]
                    ACT.op(lambda e: e.activation(out=es_ap, in_=p_s[:], func=AF.Exp, scale=SCALE),
                           reads=[r_s], writes=[r_est[i]])
                    c0 = (14 - 2 * j) * 64
                    DVE.op(lambda e: e.tensor_tensor(out=pt_ap, in0=es_ap, in1=m3r[:, h, c0:c0 + 512],
                                                     op=ALU.mult),
                           reads=[r_est[i], r_m3r], writes=[r_ptt[i]])

                cb = None
                if hg == NHG - 1 and hl == HG - 1:
                    def cb(q0=blk["q0"], oi=oi):
                        POOL.dma(aT_d[:, :, q0:q0 + 512].rearrange("h p t -> p h t"), ostg[oi][:, 0:NAH, :],
                                 reads=[r_ostg[oi]])
                g = attn_group(len(chunks), r_vw[wi], ostg[oi][:, h, :], r_ostg[oi], done_cb=cb)
                for jx in range(len(chunks)):
                    attn_step(g, jx, qk_fn, post_fn, vwin[wi][:, chunks[jx], hl * 128:(hl + 1) * 128])
                if hl == 0 and ui + 1 < len(na_units):
                    na_load(na_units[ui + 1], (ui + 1) % NWIN)
        attn_flush()

        barrier()

        cv = Carver()
        ostg = [cv.take([GQH, 512], BF16) for _ in range(2)]
        r_ostg = [Res(f"ostgb{i}") for i in range(2)]
        NPT = 8
        ptt = [cv.take([512], BF16) for _ in range(NPT)]
        r_ptt = [Res(f"pttb{i}") for i in range(NPT)]
        rec = [cv.take([512], F32) for _ in range(2)]
        r_rec = [Res(f"recb{i}") for i in range(2)]
        gkT_sb = cv.take([KVH, SEQ], BF16)
        gv_sb = cv.take([SEQ // 128, cfg.KVW], BF16)
        r_gk, r_gv = Res("gk"), Res("gv")
        accs["ap"] = [[cv.take([512], F32) for _ in range(3)] for _ in range(2)]
        accs["res"] = [[Res(f"acc{p}_{i}") for i in range(3)] for p in range(2)]
        NQT = 2
        gqt = [cv.take([GQH * 128], BF16) for _ in range(NQT)]
        r_gqt = [Res(f"gqt{i}") for i in range(NQT)]

        def gview(ap):
            return ap.rearrange("p (g t) -> p g t", t=128)

        for part in range(2):
            kv0 = 0 if part == 0 else SEQ
            q00 = 0 if part == 0 else SEQ
            nqb = (SEQ if part == 0 else HALF) // 128
            SP.dma(gkT_sb, gkT[:, :, kv0:kv0 + SEQ].rearrange("h p t -> p h t"), writes=[r_gk])
            SP.dma(gv_sb, gv[kv0:kv0 + SEQ, :].rearrange("(j p) c -> p j c", p=128), writes=[r_gv])

            def q_load(qb, q00=q00):
                qi = qb % NQT
                q0 = q00 + qb * 128
                SP.dma(gview(gqt[qi]), gqT[:, :, q0:q0 + 128].rearrange("h p t -> p h t"), writes=[r_gqt[qi]])

            q_load(0)
            for qb in range(nqb):
                if qb + 1 < nqb:
                    q_load(qb + 1)
                qi = qb % NQT
                if qb % 4 == 0:
                    oi = ostg_rr[0] % 2
                    ostg_rr[0] += 1
                for kvh in range(KVH):
                    def qk_fn(j, p_s, r_s, kvh=kvh, qi=qi):
                        PE.op(lambda e: e.matmul(p_s[:], lhsT=gkT_sb[:, kvh, j * 128:(j + 1) * 128],
                                                 rhs=gqt[qi][:, kvh * 512:(kvh + 1) * 512], start=True, stop=True),
                              reads=[r_gk, r_gqt[qi]], writes=[r_s], inc=True)

                    def post_fn(j, p_s, r_s, i):
                        pt_ap = ptt[i]
                        ACT.op(lambda e: e.activation(out=pt_ap, in_=p_s[:], func=AF.Exp, scale=SCALE),
                               reads=[r_s], writes=[r_ptt[i]])

                    cb = None
                    if qb % 4 == 3 and kvh == KVH - 1:
                        def cb(q0=q00 + (qb - 3) * 128, oi=oi):
                            POOL.dma(bT_d[:, :, q0:q0 + 512].rearrange("h p t -> p h t"), ostg[oi][:, 0:GQH, :],
                                     reads=[r_ostg[oi]])
                    o_ap = ostg[oi][:, kvh * G:(kvh + 1) * G, (qb % 4) * 128:(qb % 4 + 1) * 128]
                    g = attn_group(SEQ // 128, [r_gv], o_ap, r_ostg[oi], view=gview, done_cb=cb, den="vec")
                    for j in range(SEQ // 128):
                        attn_step(g, j, qk_fn, post_fn, gv_sb[:, j, kvh * 128:(kvh + 1) * 128])
            attn_flush()

        barrier()
        bank_mod[0], bank_mod[1] = 8, 0

        cv = Carver()
        xt = cv.take([4, D], F32)
        r_xt = [Res(f"xt3_{i}") for i in range(4)]
        yb = cv.take([4, D], F32)
        r_yb = [[Res(f"yb{s}_{n}") for n in range(NT_D)] for s in range(4)]
        ssqp = cv.take([4 * NT_D], F32)
        r_ssqp = [Res(f"ssqp{i}") for i in range(4)]
        jnk = [cv.take([512], BF16) for _ in range(2)]
        r_jnk = [Res(f"jnk{i}") for i in range(2)]
        jnk_rr = [0]
        hT = cv.take([KC, 512], BF16)
        r_hT = [Res(f"hT3_{k}") for k in range(KC)]
        NARN = 16
        arn_bytes = max(UCH * 1024, 4 * D * 2, (NAKC + GQKC) * 1024)
        arn = cv.take([arn_bytes // 2], BF16)
        r_arn = [Res(f"arn{i}") for i in range(NARN)]
        seg = arn_bytes // NARN

        def arn_res(b0, b1):
            return r_arn[b0 // seg:(b1 - 1) // seg + 1]

        uT = arn[:, 0:UCH * 512].rearrange("p (k t) -> p k t", t=512)
        xn3 = arn[:, 0:4 * D].rearrange("p (s d) -> p s d", d=D)
        aT_sb = arn[:, 0:NAKC * 512].rearrange("p (k t) -> p k t", t=512)
        bT_sb = arn[:, NAKC * 512:(NAKC + GQKC) * 512].rearrange("p (k t) -> p k t", t=512)
        gbc = cv.take([D], F32)
        r_gbc1 = Res("gbc1")
        r_uT = [arn_res(k * 1024, (k + 1) * 1024) for k in range(UCH)]
        r_xn3 = [arn_res(s * D * 2, (s + 1) * D * 2) for s in range(4)]
        r_aT = arn_res(0, NAKC * 1024)
        r_bT = arn_res(NAKC * 1024, (NAKC + GQKC) * 1024)
        ptile = cv.take([4, PLE], F32)
        r_ptile = Res("ptile")
        pbf = cv.take([4, PLE], BF16)
        r_pbf = Res("pbf")
        pT = cv.take([PKC, 512], BF16)
        r_pT = Res("pT")
        NGB = 6
        gbuf = [cv.take([2, 512], BF16) for _ in range(NGB)]
        r_gbuf = [[Res(f"gbuf{i}_{j}") for j in range(2)] for i in range(NGB)]
        NTM = 2
        tm1 = [cv.take([512], F32) for _ in range(NTM)]
        tm2 = [cv.take([512], F32) for _ in range(NTM)]
        r_tm1 = [Res(f"tm1_{i}") for i in range(NTM)]
        r_tm2 = [Res(f"tm2_{i}") for i in range(NTM)]
        rl = [cv.take([512], BF16) for _ in range(2)]
        r_rl = [Res(f"rl{i}") for i in range(2)]
        gb_rr = [0]
        tm_rr = [0]
        rl_rr = [0]

        def tokmajor_linear(actT, r_act_fn, ktiles, evac, shared_w=None):
            for n in range(NT_D):
                banks = [bank() for _ in range(4)]
                for kt, (kbase, kcount) in enumerate(ktiles):
                    if shared_w is None:
                        wt, r_wt = wget()
                    else:
                        wt, r_wt = shared_w[0][:, n * kcount:(n + 1) * kcount, :], shared_w[1]
                    for k in range(kcount):
                        for s in range(4):
                            pb, rb = banks[s]
                            first = (kt == 0 and k == 0)
                            last = (kt == len(ktiles) - 1 and k == kcount - 1)
                            PE.op(lambda e, k=k, s=s, pb=pb, wt=wt, kk=kbase + k, first=first, last=last: e.matmul(
                                pb[:], lhsT=actT[:, kk, s * 128:(s + 1) * 128], rhs=wt[:, k, :], start=first, stop=last),
                                reads=[r_wt] + r_act_fn(kbase + k), writes=[rb],
                                inc=(last or (k == kcount - 1 and s == 3)))
                for s in range(4):
                    evac(s, n, banks[s][0], banks[s][1])

        def gain_load(gain_name):
            SP.dma(gbc, gains[gain_name].partition_broadcast(128), writes=[r_gbc1])

        def blk_post(s, n):
            ji = jnk_rr[0] % 2
            jnk_rr[0] += 1
            blk = yb[:, s, n * 512:(n + 1) * 512]
            ACT.op(lambda e: e.activation(out=jnk[ji], in_=blk, func=AF.Square, scale=float(D) ** -0.5,
                                          accum_out=ssqp[:, s * NT_D + n:s * NT_D + n + 1]),
                   reads=[r_yb[s][n]], writes=[r_jnk[ji], r_ssqp[s]])
            POOL.op(lambda e: e.tensor_tensor(out=blk, in0=blk, in1=gbc[:, n * 512:(n + 1) * 512], op=ALU.mult),
                    reads=[r_yb[s][n], r_gbc1], writes=[r_yb[s][n]])

        def post_norm():
            DVE.op(lambda e: e.tensor_reduce(out=stat[:, 8:12], in_=ssqp.rearrange("p (s n) -> p s n", n=NT_D),
                                             axis=mybir.AxisListType.X, op=ALU.add),
                   reads=r_ssqp, writes=[r_stat])
            ACT.op(lambda e: e.activation(out=stat[:, 8:12], in_=stat[:, 8:12], func=AF.Sqrt, bias=EPS),
                   reads=[r_stat], writes=[r_stat])
            DVE.op(lambda e: e.reciprocal(out=stat[:, 12:16], in_=stat[:, 8:12]), reads=[r_stat], writes=[r_stat])
            for s in range(4):
                eng = DVE
                eng.op(lambda e, s=s: e.scalar_tensor_tensor(out=xt[:, s, :], in0=yb[:, s, :],
                                                             scalar=stat[:, 12 + s:13 + s], in1=xt[:, s, :],
                                                             op0=ALU.mult, op1=ALU.add),
                       reads=r_yb[s] + [r_stat, r_xt[s]], writes=[r_xt[s]])

        def p3_attn_load(t):
            t0 = t * T
            SP.dma(aT_sb, aT_d[:, :, t0:t0 + T].rearrange("h p t -> p h t"), writes=r_aT)
            SP.dma(bT_sb, bT_d[:, :, t0:t0 + T].rearrange("h p t -> p h t"), writes=r_bT)

        def p3_gate_load(t, m):
            t0 = t * T
            gi = m % NGB
            SP.dma(gbuf[gi][:, 0, :], sgaT[m, :, t0:t0 + T], writes=[r_gbuf[gi][0]])
            SP.dma(gbuf[gi][:, 1, :], sgbT[m, :, t0:t0 + T], writes=[r_gbuf[gi][1]])

        def phase3_tile(t):
            t0 = t * T
            if t == 0:
                p3_attn_load(0)
            for m0 in range(min(2, KC)):
                p3_gate_load(t, m0)
            for nt in range(NT_D):
                wa, r_wa, r_wg = wget()
                wg = wa[:, NAKC:NAKC + GQKC, :]
                for c in range(4):
                    m = nt * 4 + c
                    gi = m % NGB
                    if m + 2 < KC:
                        p3_gate_load(t, m + 2)
                    pa, ra = bank()
                    for k in range(NAKC):
                        PE.op(lambda e, k=k, c=c, pa=pa, wa=wa: e.matmul(
                            pa[:], lhsT=wa[:, k, c * 128:(c + 1) * 128], rhs=aT_sb[:, k, :], start=(k == 0),
                            stop=(k == NAKC - 1)), reads=[r_wa] + r_aT, writes=[ra], inc=(k == NAKC - 1))
                    pg, rg = bank()
                    for k in range(GQKC):
                        PE.op(lambda e, k=k, c=c, pg=pg, wg=wg: e.matmul(
                            pg[:], lhsT=wg[:, k, c * 128:(c + 1) * 128], rhs=bT_sb[:, k, :], start=(k == 0),
                            stop=(k == GQKC - 1)), reads=[r_wg] + r_bT, writes=[rg], inc=(k == GQKC - 1))
                    ti = tm_rr[0] % NTM
                    tm_rr[0] += 1
                    DVE.op(lambda e, pa=pa, gi=gi, ti=ti: e.tensor_tensor(out=tm1[ti], in0=pa[:],
                                                                         in1=gbuf[gi][:, 0, :], op=ALU.mult),
                           reads=[ra, r_gbuf[gi][0]], writes=[r_tm1[ti]])
                    DVE.op(lambda e, pg=pg, gi=gi, ti=ti: e.tensor_tensor(out=tm2[ti], in0=pg[:],
                                                                         in1=gbuf[gi][:, 1, :], op=ALU.mult),
                           reads=[rg, r_gbuf[gi][1]], writes=[r_tm2[ti]])
                    POOL.op(lambda e, m=m, ti=ti: e.tensor_tensor(out=hT[:, m, :], in0=tm1[ti], in1=tm2[ti],
                                                                  op=ALU.add),
                            reads=[r_tm1[ti], r_tm2[ti]], writes=[r_hT[m]])

            SP.dma(xt, xall[t0:t0 + T, :].rearrange("(s p) d -> p s d", p=128), writes=r_xt)
            SP.dma(ptile, pown[t0:t0 + T, :].rearrange("(s p) d -> p s d", p=128), writes=[r_ptile])

            def evac_y(s, n, pb, rb):
                evac_copy(yb[:, s, n * 512:(n + 1) * 512], pb[:], [rb], [r_yb[s][n]])
                blk_post(s, n)

            gain_load("ln_mix_post")
            tokmajor_linear(hT, lambda k: [r_hT[k]], [(0, KC)], evac_y)
            post_norm()
            norm_transpose(xt, r_xt, xn3, r_xn3, hT, r_hT, gsb["ln_mlp_pre"], sq_split=False)
            for half in range(2):
                for nt in range(FF_HALF // 512):
                    wt, r_wt = wget()
                    for c in range(4):
                        ff = nt * 4 + c
                        pb, rb = bank()
                        for k in range(KC):
                            PE.op(lambda e, k=k, c=c, pb=pb, wt=wt: e.matmul(
                                pb[:], lhsT=wt[:, k, c * 128:(c + 1) * 128], rhs=hT[:, k, :], start=(k == 0),
                                stop=(k == KC - 1)), reads=[r_wt, r_hT[k]], writes=[rb], inc=(k == KC - 1))
                        ri = rl_rr[0] % 2
                        rl_rr[0] += 1
                        ACT.op(lambda e, pb=pb, ri=ri: e.activation(out=rl[ri], in_=pb[:], func=AF.Relu),
                               reads=[rb], writes=[r_rl[ri]])
                        POOL.op(lambda e, ff=ff, ri=ri: e.tensor_tensor(out=uT[:, ff, :], in0=rl[ri],
                                                                        in1=rl[ri], op=ALU.mult),
                                reads=[r_rl[ri]], writes=r_uT[ff])

                def evac_f(s, n, pb, rb, half=half):
                    if half == 0:
                        evac_copy(yb[:, s, n * 512:(n + 1) * 512], pb[:], [rb], [r_yb[s][n]])
                    else:
                        DVE.op(lambda e: e.tensor_tensor(out=yb[:, s, n * 512:(n + 1) * 512], in0=pb[:],
                                                         in1=yb[:, s, n * 512:(n + 1) * 512], op=ALU.add),
                               reads=[rb, r_yb[s][n]], writes=[r_yb[s][n]])
                        blk_post(s, n)

                if half == 1:
                    gain_load("ln_mlp_post")
                tokmajor_linear(uT, lambda k: r_uT[k], [(kg * KGS, KGS) for kg in range(KG)], evac_f)
            post_norm()
            norm_transpose(xt, r_xt, xn3, r_xn3, hT, r_hT, gsb["ln_ple_pre"], sq_split=False)
            if t + 1 < cfg.TOK_OWN // T:
                p3_attn_load(t + 1)

            def evac_g(s, n, pb, rb):
                ACT.op(lambda e: e.activation(out=yb[:, s, n * 512:(n + 1) * 512], in_=pb[:], func=AF.Sigmoid),
                       reads=[rb], writes=[r_yb[s][n]])

            tokmajor_linear(hT, lambda k: [r_hT[k]], [(0, KC)], evac_g)
            ACT.op(lambda e: e.activation(out=pbf, in_=ptile, func=AF.Copy), reads=[r_ptile], writes=[r_pbf])
            for k in range(PKC):
                pb, rb = bank()
                pbb = pb[:].bitcast(BF16)
                for s in range(4):
                    PE.op(lambda e, s=s, k=k, pbb=pbb: e.transpose(out=pbb[:, s * 128:(s + 1) * 128],
                                                                  in_=pbf[:, s, k * 128:(k + 1) * 128],
                                                                  identity=ident[:]),
                          reads=[r_pbf], writes=[rb], inc=(s == 3))
                DVE.op(lambda e, k=k, pbb=pbb: e.tensor_copy(out=pT[:, k, :], in_=pbb[:, 0:512]), reads=[rb],
                       writes=[r_pT])

            def evac_e(s, n, pb, rb):
                DVE.op(lambda e: e.tensor_tensor(out=yb[:, s, n * 512:(n + 1) * 512], in0=pb[:],
                                                 in1=yb[:, s, n * 512:(n + 1) * 512], op=ALU.mult),
                       reads=[rb, r_yb[s][n]], writes=[r_yb[s][n]])
                blk_post(s, n)

            gain_load("ln_ple_post")
            tokmajor_linear(pT, lambda k: [r_pT], [(0, PKC)], evac_e, shared_w=wget())
            post_norm()
            POOL.dma(out[t0:t0 + T, :].rearrange("(s p) d -> p s d", p=128), xt, reads=r_xt)

        for t in range(cfg.TOK_OWN // T):
            phase3_tile(t)

        assert wstate["taken"] == len(wseq), (wstate, len(wseq))
        for P in POOL.dma_prods:
            if P.count > 0:
                POOL._need(P, P.count)
        build_program.stats = {e.name: (len(e.ops), e.prod.count) for e in engines}

        with nc.Block() as block:
            @block.sync
            def _(e):
                SP.replay(e)

            @block.gpsimd
            def _(e):
                POOL.replay(e)

            @block.scalar
            def _(e):
                ACT.replay(e)

            @block.vector
            def _(e):
                DVE.replay(e)

            @block.tensor
            def _(e):
                PE.replay(e)
    return nc


_PROGRAM_CACHE = {}


def make_in_maps(cfg, x_prompt, x_sample, p_prompt, p_sample, ln_mix_pre, w_in, q_norm, k_norm, na_rpb,
                 w_na_branch, w_gqa_branch, w_mix_out, ln_mix_post, ln_mlp_pre, w_ff1, w_ff2, ln_mlp_post,
                 ln_ple_pre, w_ple_gate, w_ple_proj, ln_ple_post):
    f32 = np.float32
    A = lambda a: np.ascontiguousarray(np.asarray(a, dtype=f32))
    x_prompt, x_sample, p_prompt, p_sample = A(x_prompt), A(x_sample), A(p_prompt)[0], A(p_sample)[0]
    D = cfg.D
    dri, dci, valid = _na_tables()
    rpb = A(na_rpb)[0]
    rpbG = np.where(valid[None], rpb[:, dri, dci], f32(0)).astype(f32)
    maskK = valid.astype(f32).astype(ml_dtypes.bfloat16)
    shared = {
        "w_in": A(w_in)[0], "w_na": A(w_na_branch)[0], "w_gqa": A(w_gqa_branch)[0], "w_mix": A(w_mix_out)[0],
        "w_ff1": A(w_ff1)[0], "w_ff2": A(w_ff2)[0], "w_gate": A(w_ple_gate)[0], "w_proj": A(w_ple_proj)[0],
        "ln_mix_pre": A(ln_mix_pre)[0], "ln_mix_post": A(ln_mix_post)[0], "ln_mlp_pre": A(ln_mlp_pre)[0],
        "ln_mlp_post": A(ln_mlp_post)[0], "ln_ple_pre": A(ln_ple_pre)[0], "ln_ple_post": A(ln_ple_post)[0],
        "q_norm": A(q_norm)[0], "k_norm": A(k_norm)[0], "rpbG": rpbG, "maskK": maskK, "A16": _A16(),
        "ident": np.eye(128, dtype=f32).astype(ml_dtypes.bfloat16), "permT": _perm_T(),
    }
    full_types = ["first"] + ["int"] * 6 + ["last"]
    rowB = [_rowB(full_types + ["first", "int", "int", "int"]), _rowB(full_types + ["int", "int", "int", "last"])]
    ropes = []
    for hidx in range(2):
        own = np.arange(hidx * HALF, (hidx + 1) * HALF)
        oth = np.arange((1 - hidx) * HALF, (2 - hidx) * HALF)
        pos = np.concatenate([np.arange(SEQ), own, oth])
        c, s = _rope_tables(pos)
        ropes.append(np.ascontiguousarray(np.stack([c, s], 0)))
    in_maps = []
    for c in range(8):
        sidx, hidx = c // 2, c % 2
        xs = x_sample[sidx]
        own = xs[hidx * HALF:(hidx + 1) * HALF]
        oth = xs[(1 - hidx) * HALF:(2 - hidx) * HALF]
        halo = np.zeros((512, D), f32)
        if hidx == 0:
            halo[256:512] = xs[HALF:HALF + 256]
        else:
            halo[0:256] = xs[HALF - 256:HALF]
        xall = np.concatenate([x_prompt[c], own, oth, halo], 0)
        pown = np.concatenate([p_prompt[c], p_sample[sidx][hidx * HALF:(hidx + 1) * HALF]], 0)
        m = dict(shared)
        m.update({"xall": np.ascontiguousarray(xall), "pown": np.ascontiguousarray(pown), "rowB": rowB[hidx],
                  "ropecs": ropes[hidx]})
        in_maps.append(m)
    return in_maps


def kernel(**inputs):
    return _run(Cfg(), inputs)


def _run(cfg, inputs):
    key = (cfg.D, cfg.NAH, cfg.GQH, cfg.KVH, cfg.DFF, cfg.PLE)
    if key not in _PROGRAM_CACHE:
        _PROGRAM_CACHE[key] = build_program(cfg)
    nc = _PROGRAM_CACHE[key]
    in_maps = make_in_maps(cfg, **inputs)
    res = run_bass_kernel_spmd(nc, in_maps, core_ids=list(range(8)))
    D = cfg.D
    y_prompt = np.empty((8, SEQ, D), np.float32)
    y_sample = np.empty((4, SEQ, D), np.float32)
    for c in range(8):
        o = np.asarray(res.results[c]["out"])
        y_prompt[c] = o[:SEQ]
        y_sample[c // 2, (c % 2) * HALF:(c % 2 + 1) * HALF] = o[SEQ:]
    return (y_prompt, y_sample)
```
